# Optimizing a Trainium2 kernel written in Bass

```python
import math
import jax
import jax.numpy as jnp
from jax import lax
import numpy as np

D_MODEL = 4096
BATCH = 2
SEQ = 8192
DEPTH = 2

D_MIX = D_MODEL
GROUP = D_MIX // 4
DIFF_HEADS = 8
DIFF_HD = GROUP // DIFF_HEADS
DIFF_QK = DIFF_HD // 2
GLA_HEADS = 4
GLA_DV = GROUP // GLA_HEADS
GLA_DK = GLA_DV // 2
GLA_RANK = 16
GLA_NORMALIZER = 16.0
GLA_CHUNK = 64
RWKV_HD = 64
RWKV_HEADS = GROUP // RWKV_HD
RWKV_W_RANK = 64
RWKV_A_RANK = 64
RWKV_G_RANK = 64
RWKV_V_RANK = 32
RWKV_LN_EPS = 64e-5
SWA_HD = 64
SWA_HEADS = GROUP // SWA_HD
SWA_KV_HEADS = 2
SWA_WINDOW = 128
SWA_BLOCK = 128
REL_BUCKETS = 32
REL_MAX_DIST = 128
REL_HEADS = DIFF_HEADS + SWA_HEADS
ATTN_BLOCK = 128
D_FF = -(-8 * D_MODEL // (3 * 256)) * 256
NORM_EPS = 1e-6
NEG_INF = -1e30
DIFF_COLS = 3 * GROUP
GLA_COLS = 2 * GLA_HEADS * GLA_DK + 2 * GROUP + GLA_RANK
RWKV_COLS = 3 * GROUP + RWKV_W_RANK + RWKV_A_RANK + RWKV_G_RANK
SWA_COLS = GROUP + 2 * SWA_KV_HEADS * SWA_HD
P_IN = DIFF_COLS + GLA_COLS + RWKV_COLS + SWA_COLS

kernel_name = "hybrid_parallel_heads_diff_gla_rwkv7_swa"


def _rms(t, eps=NORM_EPS):
    tf = t.astype(jnp.float32)
    return (tf * lax.rsqrt(jnp.mean(tf * tf, axis=-1, keepdims=True) + eps)).astype(t.dtype)


def _t5_bucket(dist):
    max_exact = REL_BUCKETS // 2
    n = jnp.maximum(dist, 0)
    nf = jnp.maximum(n, 1).astype(jnp.float32)
    large = max_exact + (jnp.log(nf / max_exact) / math.log(REL_MAX_DIST / max_exact)
                         * (REL_BUCKETS - max_exact)).astype(jnp.int32)
    large = jnp.minimum(large, REL_BUCKETS - 1)
    return jnp.where(n < max_exact, n, large)


def _token_shift(t, mu):
    prev = jnp.pad(t, ((0, 0), (1, 0), (0, 0)))[:, :-1]
    return t + (prev - t) * mu


def _diff_attention(cols, layer_idx, q_gain, k_gain, lam, subln, bias_tab):
    bsz, seq, _ = cols.shape
    q, k, v = jnp.split(cols, [GROUP, 2 * GROUP], axis=-1)
    q = _rms(q.reshape(bsz, seq, DIFF_HEADS, 2, DIFF_QK)) * q_gain
    k = _rms(k.reshape(bsz, seq, DIFF_HEADS, 2, DIFF_QK)) * k_gain
    v = v.reshape(bsz, seq, DIFF_HEADS, DIFF_HD)
    lam_init = 0.8 - 0.6 * math.exp(-0.3 * layer_idx)
    lam = lam.astype(jnp.float32)
    lam_full = jnp.exp(jnp.sum(lam[0] * lam[1])) - jnp.exp(jnp.sum(lam[2] * lam[3])) + lam_init
    n_blk = seq // ATTN_BLOCK
    q_blocks = q.reshape(bsz, n_blk, ATTN_BLOCK, DIFF_HEADS, 2, DIFF_QK).transpose(1, 0, 2, 3, 4, 5)
    k_pos = jnp.arange(seq)
    scale = DIFF_QK ** -0.5

    def one_block(args):
        q_i, i = args
        s = jnp.einsum("bqhpd,bkhpd->bphqk", q_i, k).astype(jnp.float32) * scale
        dist = (i * ATTN_BLOCK + jnp.arange(ATTN_BLOCK))[:, None] - k_pos[None, :]
        bias = bias_tab[_t5_bucket(dist)].astype(jnp.float32).transpose(2, 0, 1)
        s = jnp.where(dist >= 0, s + bias, NEG_INF)
        p = jax.nn.softmax(s, axis=-1)
        attn = p[:, 0] - lam_full * p[:, 1]
        return jnp.einsum("bhqk,bkhd->bqhd", attn.astype(v.dtype), v)

    o = lax.map(one_block, (q_blocks, jnp.arange(n_blk)))
    o = o.transpose(1, 0, 2, 3, 4).reshape(bsz, seq, DIFF_HEADS, DIFF_HD)
    o = _rms(o) * subln * (1.0 - lam_init)
    return o.reshape(bsz, seq, GROUP)


def _gla(cols, gate_up, gate_bias, out_gain):
    bsz, seq, _ = cols.shape
    f32 = jnp.float32
    kd = GLA_HEADS * GLA_DK
    q, k, v, g, gd = jnp.split(cols, np.cumsum([kd, kd, GROUP, GROUP]).tolist(), axis=-1)
    log_a = jax.nn.log_sigmoid((gd @ gate_up + gate_bias).astype(f32)) / GLA_NORMALIZER
    n = seq // GLA_CHUNK

    def heads(t, d):
        return (t.astype(f32).reshape(bsz, seq, GLA_HEADS, d).transpose(0, 2, 1, 3)
                .reshape(bsz, GLA_HEADS, n, GLA_CHUNK, d))

    q = heads(q, GLA_DK) * GLA_DK ** -0.5
    k = heads(k, GLA_DK)
    v = heads(v, GLA_DV)
    b = jnp.cumsum(heads(log_a, GLA_DK), axis=3)
    b_last = b[:, :, :, -1:]
    q_dec = q * jnp.exp(b)
    causal = jnp.tril(jnp.ones((GLA_CHUNK, GLA_CHUNK), dtype=bool))
    a_intra = jnp.where(causal, jnp.einsum("bhncd,bhnsd->bhncs", q_dec, k * jnp.exp(-b)), 0.0)
    o = jnp.einsum("bhncs,bhnsv->bhncv", a_intra, v)
    upd = jnp.einsum("bhnsd,bhnsv->bhndv", k * jnp.exp(b_last - b), v)
    chunk_decay = jnp.exp(b_last[:, :, :, 0])

    def step(state, inp):
        dec, u = inp
        return dec[..., None] * state + u, state

    _, s_prev = lax.scan(step, jnp.zeros((bsz, GLA_HEADS, GLA_DK, GLA_DV), f32),
                         (jnp.moveaxis(chunk_decay, 2, 0), jnp.moveaxis(upd, 2, 0)))
    s_prev = jnp.moveaxis(s_prev, 0, 2)
    o = o + jnp.einsum("bhncd,bhndv->bhncv", q_dec, s_prev)
    o = o.reshape(bsz, GLA_HEADS, seq, GLA_DV).transpose(0, 2, 1, 3)
    o = _rms(o) * out_gain
    return (o.reshape(bsz, seq, GROUP) * jax.nn.silu(g.astype(f32))).astype(cols.dtype)


def _rwkv7_scan(r, w, k, v, a, b):
    bsz, seq, nh, hd = r.shape
    xs = tuple(jnp.moveaxis(t, 1, 0) for t in (r, w, k, v, a, b))

    def step(state, inp):
        r_t, w_t, k_t, v_t, a_t, b_t = inp
        sa = jnp.einsum("bhij,bhj->bhi", state, a_t)
        state = (state * w_t[:, :, None, :] + sa[..., None] * b_t[:, :, None, :]
                 + v_t[..., None] * k_t[:, :, None, :])
        return state, jnp.einsum("bhij,bhj->bhi", state, r_t)

    _, y = lax.scan(step, jnp.zeros((bsz, nh, hd, hd), jnp.float32), xs)
    return jnp.moveaxis(y, 0, 1)


def _rwkv7_time_mix(cols, xn, v_first, vres, mu, w_up, w0, a_up, a0, g_up,
                    k_k, k_a, r_k, ln_w, ln_b):
    bsz, seq, _ = cols.shape
    f32 = jnp.float32
    cols = _token_shift(cols, mu)
    r, k, v, wd, ad, gd = jnp.split(
        cols, np.cumsum([GROUP, GROUP, GROUP, RWKV_W_RANK, RWKV_A_RANK]).tolist(), axis=-1)
    if vres is None:
        v_first = v
    else:
        vres_down, vres_mu, vres_up, v0 = vres
        v_gate = jax.nn.sigmoid(v0 + _token_shift(xn @ vres_down, vres_mu) @ vres_up)
        v = v + (v_first - v) * v_gate
    w_log = -jax.nn.softplus(-(w0 + jnp.tanh(wd) @ w_up).astype(f32)) - 0.5
    decay = jnp.exp(-jnp.exp(w_log))
    a = jax.nn.sigmoid((a0 + ad @ a_up).astype(f32))
    g = jax.nn.sigmoid(gd) @ g_up
    k = k.astype(f32)

    def heads(t):
        return t.astype(f32).reshape(bsz, seq, RWKV_HEADS, RWKV_HD)

    kk = heads(k * k_k)
    kk = kk / jnp.maximum(jnp.sqrt(jnp.sum(kk * kk, axis=-1, keepdims=True)), 1e-12)
    k_mod = heads(k * (1.0 + (a - 1.0) * k_a))
    a_h = heads(a)
    r_h = heads(r)
    v_h = heads(v)
    y = _rwkv7_scan(r_h, heads(decay), k_mod, v_h, -kk, kk * a_h)
    mean = jnp.mean(y, axis=-1, keepdims=True)
    var = jnp.mean(jnp.square(y - mean), axis=-1, keepdims=True)
    y = ((y - mean) * lax.rsqrt(var + RWKV_LN_EPS)).reshape(bsz, seq, GROUP) * ln_w + ln_b
    bonus = jnp.sum(r_h * k_mod * r_k, axis=-1, keepdims=True) * v_h
    out = (y + bonus.reshape(bsz, seq, GROUP)) * g
    return out.astype(cols.dtype), v_first


def _swa_attention(cols, q_gain, k_gain, sinks, bias_tab):
    bsz, seq, _ = cols.shape
    f32 = jnp.float32
    grp = SWA_HEADS // SWA_KV_HEADS
    n_blk = seq // SWA_BLOCK
    q, k, v = jnp.split(cols, [GROUP, GROUP + SWA_KV_HEADS * SWA_HD], axis=-1)
    q = _rms(q.reshape(bsz, seq, SWA_HEADS, SWA_HD)) * q_gain
    k = _rms(k.reshape(bsz, seq, SWA_KV_HEADS, SWA_HD)) * k_gain
    v = v.reshape(bsz, seq, SWA_KV_HEADS, SWA_HD)
    q = q.reshape(bsz, n_blk, SWA_BLOCK, SWA_KV_HEADS, grp, SWA_HD)

    def band(t):
        tp = jnp.pad(t, ((0, 0), (SWA_BLOCK, 0), (0, 0), (0, 0)))
        tp = tp.reshape(bsz, n_blk + 1, SWA_BLOCK, SWA_KV_HEADS, SWA_HD)
        return jnp.concatenate([tp[:, :-1], tp[:, 1:]], axis=2)

    kb, vb = band(k), band(v)
    s = jnp.einsum("bnqhgd,bnkhd->bnhgqk", q, kb).astype(f32) * SWA_HD ** -0.5
    qi = jnp.arange(SWA_BLOCK)
    kj = jnp.arange(2 * SWA_BLOCK)
    dist = SWA_BLOCK + qi[:, None] - kj[None, :]
    k_pos = (jnp.arange(n_blk)[:, None] - 1) * SWA_BLOCK + kj[None, :]
    valid = ((dist >= 0) & (dist < SWA_WINDOW))[None] & (k_pos >= 0)[:, None, :]
    bias = (bias_tab[_t5_bucket(dist)].astype(f32)
            .reshape(SWA_BLOCK, 2 * SWA_BLOCK, SWA_KV_HEADS, grp).transpose(2, 3, 0, 1))
    s = jnp.where(valid[None, :, None, None], s + bias, NEG_INF)
    sink = sinks.astype(f32).reshape(SWA_KV_HEADS, grp)[:, :, None, None]
    m = jnp.maximum(jnp.max(s, axis=-1, keepdims=True), sink)
    p = jnp.exp(s - m)
    p = p / (jnp.sum(p, axis=-1, keepdims=True) + jnp.exp(sink - m))
    o = jnp.einsum("bnhgqk,bnkhd->bnqhgd", p.astype(v.dtype), vb)
    return o.reshape(bsz, seq, GROUP)


def _hybrid_layer(x, c, layer_idx, v_first, vres, bias_a, bias_d, p):
    mod = (jax.nn.silu(c) @ p["ada_w"] + p["ada_b"])[:, None, :]
    sh1, sc1, g1, sh2, sc2, g2 = jnp.split(mod, 6, axis=-1)
    h = _rms(x) * (1.0 + sc1) + sh1
    cols_a, cols_b, cols_c, cols_d = jnp.split(
        h @ p["w_in"], np.cumsum([DIFF_COLS, GLA_COLS, RWKV_COLS]).tolist(), axis=-1)
    o_a = _diff_attention(cols_a, layer_idx, p["diff_q_norm"], p["diff_k_norm"],
                          p["diff_lambda"], p["diff_subln"], bias_a)
    o_b = _gla(cols_b, p["gla_gate_up"], p["gla_gate_bias"], p["gla_out_norm"])
    o_c, v_first = _rwkv7_time_mix(cols_c, h, v_first, vres, p["rwkv_mu"], p["rwkv_w_up"],
                                   p["rwkv_w0"], p["rwkv_a_up"], p["rwkv_a0"], p["rwkv_g_up"],
                                   p["rwkv_k_k"], p["rwkv_k_a"], p["rwkv_r_k"],
                                   p["rwkv_ln_w"], p["rwkv_ln_b"])
    o_d = _swa_attention(cols_d, p["swa_q_norm"], p["swa_k_norm"], p["swa_sinks"], bias_d)
    mixed = jnp.concatenate([o_a, o_b, o_c, o_d], axis=-1)
    x = x + g1 * (mixed @ p["w_out"])
    h = _rms(x) * (1.0 + sc2) + sh2
    x = x + g2 * ((jax.nn.silu(h @ p["ffn_w1"]) * (h @ p["ffn_w3"])) @ p["ffn_w2"])
    return x, v_first


def setup_inputs(seed: int = 0) -> dict:
    key = jax.random.key(seed)
    ks = iter(jax.random.split(key, 40))
    L, D = DEPTH, D_MODEL
    f32 = jnp.float32

    def nrm(shape, scale):
        return jax.random.normal(next(ks), shape, f32) * scale

    def gain(shape):
        return 1.0 + nrm(shape, 0.1)

    return {
        "x": nrm((BATCH, SEQ, D), 1.0),
        "c": nrm((BATCH, D), 1.0),
        "rel_bias": nrm((REL_BUCKETS, REL_HEADS), 0.3),
        "ada_w": nrm((L, D, 6 * D), 0.5 * D ** -0.5),
        "ada_b": nrm((L, 6 * D), 0.02),
        "w_in": nrm((L, D, P_IN), D ** -0.5),
        "w_out": nrm((L, D_MIX, D), D_MIX ** -0.5),
        "diff_q_norm": gain((L, DIFF_QK)),
        "diff_k_norm": gain((L, DIFF_QK)),
        "diff_lambda": nrm((L, 4, DIFF_QK), 0.1),
        "diff_subln": gain((L, DIFF_HD)),
        "gla_gate_up": nrm((L, GLA_RANK, GLA_HEADS * GLA_DK), GLA_RANK ** -0.5),
        "gla_gate_bias": nrm((L, GLA_HEADS * GLA_DK), 0.1),
        "gla_out_norm": gain((L, GLA_DV)),
        "rwkv_mu": jax.random.uniform(next(ks), (L, RWKV_COLS), f32),
        "rwkv_w_up": nrm((L, RWKV_W_RANK, GROUP), RWKV_W_RANK ** -0.5),
        "rwkv_w0": nrm((L, GROUP), 0.5),
        "rwkv_a_up": nrm((L, RWKV_A_RANK, GROUP), RWKV_A_RANK ** -0.5),
        "rwkv_a0": nrm((L, GROUP), 0.1),
        "rwkv_g_up": nrm((L, RWKV_G_RANK, GROUP), RWKV_G_RANK ** -0.5),
        "rwkv_k_k": 0.85 + nrm((L, GROUP), 0.05),
        "rwkv_k_a": gain((L, GROUP)),
        "rwkv_r_k": nrm((L, RWKV_HEADS, RWKV_HD), 0.1),
        "rwkv_ln_w": gain((L, GROUP)),
        "rwkv_ln_b": nrm((L, GROUP), 0.02),
        "rwkv_vres_down": nrm((L - 1, D, RWKV_V_RANK), D ** -0.5),
        "rwkv_vres_mu": jax.random.uniform(next(ks), (L - 1, RWKV_V_RANK), f32),
        "rwkv_vres_up": nrm((L - 1, RWKV_V_RANK, GROUP), RWKV_V_RANK ** -0.5),
        "rwkv_v0": nrm((L - 1, GROUP), 0.5),
        "swa_q_norm": gain((L, SWA_HD)),
        "swa_k_norm": gain((L, SWA_HD)),
        "swa_sinks": nrm((L, SWA_HEADS), 0.5),
        "ffn_w1": nrm((L, D, D_FF), D ** -0.5),
        "ffn_w3": nrm((L, D, D_FF), D ** -0.5),
        "ffn_w2": nrm((L, D_FF, D), D_FF ** -0.5),
    }


def reference(x, c, rel_bias, ada_w, ada_b, w_in, w_out, diff_q_norm, diff_k_norm,
              diff_lambda, diff_subln, gla_gate_up, gla_gate_bias, gla_out_norm, rwkv_mu,
              rwkv_w_up, rwkv_w0, rwkv_a_up, rwkv_a0, rwkv_g_up, rwkv_k_k, rwkv_k_a, rwkv_r_k,
              rwkv_ln_w, rwkv_ln_b, rwkv_vres_down, rwkv_vres_mu, rwkv_vres_up, rwkv_v0,
              swa_q_norm, swa_k_norm, swa_sinks, ffn_w1, ffn_w3, ffn_w2):
    bias_a = rel_bias[:, :DIFF_HEADS]
    bias_d = rel_bias[:, DIFF_HEADS:]
    v_first = None
    for l in range(DEPTH):
        p = dict(ada_w=ada_w[l], ada_b=ada_b[l], w_in=w_in[l], w_out=w_out[l],
                 diff_q_norm=diff_q_norm[l], diff_k_norm=diff_k_norm[l],
                 diff_lambda=diff_lambda[l], diff_subln=diff_subln[l],
                 gla_gate_up=gla_gate_up[l], gla_gate_bias=gla_gate_bias[l],
                 gla_out_norm=gla_out_norm[l], rwkv_mu=rwkv_mu[l], rwkv_w_up=rwkv_w_up[l],
                 rwkv_w0=rwkv_w0[l], rwkv_a_up=rwkv_a_up[l], rwkv_a0=rwkv_a0[l],
                 rwkv_g_up=rwkv_g_up[l], rwkv_k_k=rwkv_k_k[l], rwkv_k_a=rwkv_k_a[l],
                 rwkv_r_k=rwkv_r_k[l], rwkv_ln_w=rwkv_ln_w[l], rwkv_ln_b=rwkv_ln_b[l],
                 swa_q_norm=swa_q_norm[l], swa_k_norm=swa_k_norm[l], swa_sinks=swa_sinks[l],
                 ffn_w1=ffn_w1[l], ffn_w3=ffn_w3[l], ffn_w2=ffn_w2[l])
        if l == 0:
            vres = None
        else:
            vres = (rwkv_vres_down[l - 1], rwkv_vres_mu[l - 1], rwkv_vres_up[l - 1], rwkv_v0[l - 1])
        x, v_first = _hybrid_layer(x, c, l, v_first, vres, bias_a, bias_d, p)
    return x
```

```python
import math
import numpy as np
import ml_dtypes
from contextlib import ExitStack
import concourse.bass as bass
import concourse.mybir as mybir
from concourse.bass_utils import run_bass_kernel_spmd

F32 = mybir.dt.float32
BF16 = mybir.dt.bfloat16
AF = mybir.ActivationFunctionType
ALU = mybir.AluOpType
AX = mybir.AxisListType
NPBF = ml_dtypes.bfloat16

D = 4096
DFF = 11008
NCORES = 8
EPS = 1e-6
SAME_ENGINE_SYNC = True


class St:
    __slots__ = ("w", "r", "multi")

    def __init__(self, multi):
        self.w = {}
        self.r = {}
        self.multi = multi


class T:
    __slots__ = ("h", "s", "name")

    def __init__(self, h, name, multi=False, s=None):
        self.h = h
        self.s = s if s is not None else St(multi)
        self.name = name

    def __getitem__(self, idx):
        return self.h[idx]

    def view(self, ap):
        return T(ap, self.name, s=self.s)


class K:
    NSLOT = 12

    def __init__(self):
        self.nc = bass.Bass("TRN2", target_bir_lowering=False)
        self.es = ExitStack()
        nc = self.nc
        self.E = {"pe": nc.tensor, "dve": nc.vector, "act": nc.scalar, "pool": nc.gpsimd, "sp": nc.sync}
        self.sem = {}
        self.cnt = {}
        for e in ("pe", "dve", "act", "pool"):
            self.sem[e] = self.es.enter_context(nc.semaphore("s_" + e))
            self.cnt[e] = 0
        self.waited = {e: {} for e in self.E}
        self.dq = {}
        for q in ("sp", "pool", "act"):
            slots = []
            for i in range(self.NSLOT):
                key = ("d", q, i)
                self.sem[key] = self.es.enter_context(nc.semaphore("d_%s%d" % (q, i)))
                slots.append(0)
            self.dq[q] = {"rr": 0, "val": slots}
        self.outs = []
        self.n_ins = 0
        self.uid = 0
        self.stack = [self.es]
        self.stq = "pool"
        self.aux = "pool"
        self.sem["cc"] = self.es.enter_context(nc.semaphore("s_cc"))
        self.ccval = 0

    def sb(self, name, shape, dt):
        self.uid += 1
        h = self.stack[-1].enter_context(self.nc.sbuf_tensor("%s_%d" % (name, self.uid), list(shape), dt))
        return T(h, name)

    def ps(self, name, shape, dt=F32):
        self.uid += 1
        h = self.stack[-1].enter_context(self.nc.psum_tensor("%s_%d" % (name, self.uid), list(shape), dt))
        return T(h, name)

    def dram(self, name, shape, dt, kind):
        kinds = {"in": "ExternalInput", "out": "ExternalOutput"}
        h = self.nc.dram_tensor(name, list(shape), dt, kind=kinds[kind]).ap()
        return T(h, name, multi=True)

    def dint(self, name, shape, dt):
        return T(self.nc.dram_tensor(name, list(shape), dt).ap(), name, multi=True)

    def push(self):
        es = ExitStack()
        self.stack.append(es)

    def pop(self):
        self.barrier()
        self.stack.pop().close()

    def barrier(self):
        evs = [(e, self.cnt[e]) for e in ("pe", "dve", "act", "pool") if self.cnt[e]]
        for q in self.dq:
            for slot, val in enumerate(self.dq[q]["val"]):
                if val:
                    evs.append((("d", q, slot), val))
        if self.ccval:
            evs.append(("cc", self.ccval))
        for eng in ("pe", "dve", "act", "pool", "sp"):
            for ev in evs:
                if ev[0] != eng:
                    self._wait(eng, ev)

    def _wait(self, eng, ev):
        if ev is None:
            return
        key, val = ev
        if key == eng and (eng == "pe" or not SAME_ENGINE_SYNC):
            return
        if self.waited[eng].get(key, 0) >= val:
            return
        self.E[eng].wait_ge(self.sem[key], val)
        self.waited[eng][key] = val

    def _deps(self, eng, r, w):
        for t in r:
            for key, val in t.s.w.items():
                self._wait(eng, (key, val))
        for t in w:
            for key, val in t.s.w.items():
                self._wait(eng, (key, val))
            for key, val in t.s.r.items():
                self._wait(eng, (key, val))

    def _mark(self, ev, r, w):
        for t in r:
            if t.s.r.get(ev[0], 0) < ev[1]:
                t.s.r[ev[0]] = ev[1]
        for t in w:
            if t.s.multi:
                if t.s.w.get(ev[0], 0) < ev[1]:
                    t.s.w[ev[0]] = ev[1]
            else:
                t.s.w = {ev[0]: ev[1]}
                t.s.r = {}

    def op(self, eng, fn, r=(), w=()):
        self._deps(eng, r, w)
        ins = fn(self.E[eng])
        self.cnt[eng] += 1
        ins.then_inc(self.sem[eng], 1)
        self._mark((eng, self.cnt[eng]), r, w)
        self.n_ins += 1

    def dma(self, q, out_ap, in_ap, r=(), w=(), slow=False):
        self._deps(q, r, w)
        st = self.dq[q]
        slot = st["rr"]
        st["rr"] = (slot + 1) % self.NSLOT
        key = ("d", q, slot)
        if st["val"][slot]:
            self._wait(q, (key, st["val"][slot]))
        if slow:
            ins = self.E[q].dma_start(out=out_ap, in_=in_ap, allow_slow_non_contiguous=True)
        else:
            ins = self.E[q].dma_start(out=out_ap, in_=in_ap)
        ins.then_inc(self.sem[key], 16)
        st["val"][slot] += 16
        self._mark((key, st["val"][slot]), r, w)
        self.n_ins += 1

    def cc(self, kind, groups, in_t, in_ap, out_t, out_ap):
        q = "pool"
        self._deps(q, [in_t], [out_t])
        ins = self.nc.gpsimd.collective_compute(kind, ALU.bypass, replica_groups=groups, ins=[in_ap], outs=[out_ap])
        ins.then_inc(self.sem["cc"], 1)
        self.ccval += 1
        self._mark(("cc", self.ccval), [in_t], [out_t])

    def finish(self):
        for q in self.dq:
            for slot, val in enumerate(self.dq[q]["val"]):
                if val:
                    self._wait("sp", (("d", q, slot), val))
        if self.ccval:
            self._wait("sp", ("cc", self.ccval))
        while len(self.stack) > 1:
            self.stack.pop().close()
        self.es.close()
        return self.nc

    def mm(self, out_t, out_ap, lt, l_ap, rt, r_ap, start, stop):
        self.op("pe", lambda e: e.matmul(out_ap, l_ap, r_ap, start=start, stop=stop), r=[lt, rt], w=[out_t])


class Rot:
    def __init__(self, tiles):
        self.tiles = tiles
        self.i = 0

    def next(self):
        t = self.tiles[self.i]
        self.i = (self.i + 1) % len(self.tiles)
        return t


def run(nc, in_maps):
    res = run_bass_kernel_spmd(nc, in_maps, core_ids=list(range(len(in_maps))))
    return res.results


def build_cast(M, CH=4096):
    k = K()
    src = k.dram("src", [128, M], F32, "in")
    dst = k.dram("dst", [128, M], BF16, "out")
    ins = Rot([k.sb("cin", [128, CH], F32) for _ in range(3)])
    outs = Rot([k.sb("cout", [128, CH], BF16) for _ in range(3)])
    assert M % CH == 0
    for i in range(M // CH):
        a = ins.next()
        b = outs.next()
        k.dma("sp", a[:, :], src[:, i * CH:(i + 1) * CH], w=[a])
        if i % 2 == 0:
            k.op("dve", lambda e: e.tensor_copy(out=b[:, :], in_=a[:, :]), r=[a], w=[b])
        else:
            k.op("act", lambda e: e.activation(out=b[:, :], in_=a[:, :], func=AF.Copy), r=[a], w=[b])
        k.dma("pool", dst[:, i * CH:(i + 1) * CH], b[:, :], r=[b])
    return k.finish()


def tile_weight(w):
    Kd, N = w.shape
    KC, NT = Kd // 128, N // 128
    return np.ascontiguousarray(w.reshape(KC, 128, NT, 128).transpose(2, 1, 0, 3)).reshape(NT, 128, KC * 128)


def cast_weights(arrs):
    CH = 4096
    flat = [a.reshape(-1) for a in arrs]
    tot = sum(f.size for f in flat)
    unit = NCORES * 128 * CH
    padded = -(-tot // unit) * unit
    buf = np.zeros(padded, np.float32)
    o = 0
    for f in flat:
        buf[o:o + f.size] = f
        o += f.size
    M = padded // (NCORES * 128)
    nc = build_cast(M, CH)
    parts = buf.reshape(NCORES, 128, M)
    res = run(nc, [{"src": parts[i]} for i in range(NCORES)])
    ob = np.concatenate([np.asarray(r["dst"]).reshape(-1) for r in res])
    outs = []
    o = 0
    for a in arrs:
        outs.append(ob[o:o + a.size].reshape(a.shape))
        o += a.size
    return outs


def build_ada(NCOL=6144):
    k = K()
    cT = k.dram("cT", [128, 64], F32, "in")
    w = k.dram("w", [D, NCOL], F32, "in")
    bias = k.dram("bias", [2, NCOL], F32, "in")
    out = k.dram("mod", [2, NCOL], F32, "out")
    ct = k.sb("ct", [128, 64], F32)
    st = k.sb("st", [128, 64], F32)
    bt = k.sb("bt", [2, NCOL], F32)
    ot = k.sb("ot", [2, NCOL], F32)
    k.dma("sp", ct[:, :], cT[:, :], w=[ct])
    k.dma("sp", bt[:, :], bias[:, :], w=[bt])
    k.op("act", lambda e: e.activation(out=st[:, :], in_=ct[:, :], func=AF.Silu), r=[ct], w=[st])
    wv = w.h.rearrange("(kc p) n -> p kc n", p=128)
    wts = Rot([k.sb("wt", [128, 8, 512], F32) for _ in range(3)])
    pss = Rot([k.ps("pa", [2, 512]) for _ in range(2)])
    for nb in range(NCOL // 512):
        ps = pss.next()
        for g in range(4):
            wt = wts.next()
            k.dma("sp" if g % 2 == 0 else "pool", wt[:, :, :], wv[:, g * 8:(g + 1) * 8, nb * 512:(nb + 1) * 512], w=[wt])
            for j in range(8):
                kc = g * 8 + j
                k.mm(ps, ps[:, :], st, st[:, kc * 2:kc * 2 + 2], wt, wt[:, j, :], kc == 0, kc == 31)
        sl = slice(nb * 512, (nb + 1) * 512)
        k.op("dve", lambda e: e.tensor_tensor(out=ot[:, sl], in0=ps[:, :], in1=bt[:, sl], op=ALU.add), r=[ps, bt], w=[ot])
    k.dma("sp", out[:, :], ot[:, :], r=[ot])
    return k.finish()


def run_ada(c, ada_w, ada_b):
    nc = build_ada()
    cT = np.ascontiguousarray(c.reshape(2, 32, 128).transpose(2, 1, 0)).reshape(128, 64)
    maps = []
    for i in range(NCORES):
        l, q = i // 4, i % 4
        sl = slice(q * 6144, (q + 1) * 6144)
        maps.append({"cT": cT, "w": np.ascontiguousarray(ada_w[l][:, sl]),
                     "bias": np.ascontiguousarray(np.broadcast_to(ada_b[l][sl], (2, 6144)))})
    res = run(nc, maps)
    mod = np.zeros((2, 2, 6 * D), np.float32)
    for i in range(NCORES):
        l, q = i // 4, i % 4
        mod[l, :, q * 6144:(q + 1) * 6144] = res[i]["mod"]
    return mod


def mod_layout(m):
    return np.ascontiguousarray(m.reshape(6, 32, 128).transpose(2, 0, 1)).reshape(128, 192)


def emit_norm_mod(k, TB, load_chunk, hT, s_col, b_col, ones_mean, ps_stat, sqs, tmps, rstd):
    for kc in range(32):
        xt, xap = load_chunk(kc)
        sq = sqs.next()
        k.op("act", lambda e: e.activation(out=sq[:, :], in_=xap, func=AF.Square), r=[xt], w=[sq])
        k.mm(ps_stat, ps_stat[:, :], ones_mean, ones_mean[:, :], sq, sq[:, :], kc == 0, kc == 31)
    emit_rsqrt(k, rstd, rstd[:, :], ps_stat, ps_stat[:, :], EPS, tmps)
    for kc in range(32):
        xt, xap = load_chunk(kc)
        tmp = tmps.next()
        k.op("dve", lambda e: e.tensor_tensor(out=tmp[:, :], in0=xap, in1=rstd[:, :], op=ALU.mult), r=[xt, rstd], w=[tmp])
        k.op("act", lambda e: e.activation(out=hT[:, kc, :], in_=tmp[:, :], func=AF.Identity,
                                           scale=s_col(kc), bias=b_col(kc)), r=[tmp], w=[hT])


def emit_rsqrt(k, out_t, out_ap, in_t, in_ap, eps, tmps, shape=None):
    tmp = tmps.next()
    tap = tmp[:, :] if shape is None else shape(tmp)
    k.op("act", lambda e: e.activation(out=tap, in_=in_ap, func=AF.Sqrt, bias=eps_ap(k, eps, tap), scale=1.0), r=[in_t], w=[tmp])
    k.op("dve", lambda e: e.reciprocal(out=out_ap, in_=tap), r=[tmp], w=[out_t])


def eps_ap(k, eps, like_ap):
    key = ("eps", float(eps))
    if not hasattr(k, "_consts"):
        k._consts = {}
    if key not in k._consts:
        t = k.sb("epsc", [128, 1], F32)
        k.op("pool", lambda e: e.memset(t[:, :], float(eps)), w=[t])
        k._consts[key] = t
    t = k._consts[key]
    np_ = like_ap.partition_size() if hasattr(like_ap, "partition_size") else 128
    return t[0:np_, :]


def load_mod(k, mod):
    mt = k.sb("modt", [128, 192], F32)
    k.dma("sp", mt[:, :], mod[:, :], w=[mt])
    for comp in (1, 4):
        sl = slice(comp * 32, (comp + 1) * 32)
        k.op("dve", lambda e: e.tensor_scalar_add(out=mt[:, sl], in0=mt[:, sl], scalar1=1.0), r=[mt], w=[mt])
    return mt


def make_ones(k, val, shape=(128, 128), dt=F32):
    t = k.sb("ones", list(shape), dt)
    k.op(k.aux, lambda e: e.memset(t[:, :], val), w=[t])
    return t


def build_inproj(TOK, NT, TB=512):
    k = K()
    xT = k.dram("xT", [D, TOK], F32, "in")
    mod = k.dram("mod", [128, 192], F32, "in")
    w = k.dram("w", [NT, 128, 4096], BF16, "in")
    out = k.dram("colsT", [NT * 128, TOK], F32, "out")
    mt = load_mod(k, mod)
    ones_mean = make_ones(k, 1.0 / D)
    xv = xT.h.rearrange("(kc p) t -> p kc t", p=128)
    xt = k.sb("xt", [128, 32, TB], F32)
    hT = k.sb("hT", [128, 32, TB], BF16)
    rstd = k.sb("rstd", [128, TB], F32)
    ps_stat = k.ps("ps_stat", [128, TB])
    sqs = Rot([k.sb("sq", [128, TB], F32) for _ in range(2)])
    tmps = Rot([k.sb("tmp", [128, TB], F32) for _ in range(2)])
    wts = Rot([k.sb("wt", [128, 32, 128], BF16) for _ in range(4)])
    pss = Rot([k.ps("pg", [128, TB]) for _ in range(4)])
    evs = Rot([k.sb("ev", [128, TB], F32) for _ in range(3)])
    for tb in range(TOK // TB):
        ts = slice(tb * TB, (tb + 1) * TB)
        for g in range(4):
            k.dma("sp" if g % 2 == 0 else "pool", xt[:, g * 8:(g + 1) * 8, :], xv[:, g * 8:(g + 1) * 8, ts], w=[xt])
        emit_norm_mod(k, TB, lambda kc: (xt, xt[:, kc, :]), hT,
                      lambda kc: mt[:, 32 + kc:33 + kc], lambda kc: mt[:, kc:kc + 1],
                      ones_mean, ps_stat, sqs, tmps, rstd)
        for nt in range(NT):
            wt = wts.next()
            k.dma("sp", wt[:, :, :], w.h[nt].rearrange("p (kc n) -> p kc n", n=128), w=[wt])
            ps = pss.next()
            for kc in range(32):
                k.mm(ps, ps[:, :], wt, wt[:, kc, :], hT, hT[:, kc, :], kc == 0, kc == 31)
            ev = evs.next()
            if nt % 2 == 0:
                k.op("dve", lambda e: e.tensor_copy(out=ev[:, :], in_=ps[:, :]), r=[ps], w=[ev])
            else:
                k.op("act", lambda e: e.activation(out=ev[:, :], in_=ps[:, :], func=AF.Copy), r=[ps], w=[ev])
            k.dma("pool", out[nt * 128:(nt + 1) * 128, ts], ev[:, :], r=[ev])
    return k.finish()


def build_ffn(TOK, TB=512, NFT=DFF // 128):
    k = K()
    io = {"xT": k.dram("xT", [D, TOK], F32, "in"), "mixT": k.dram("mixT", [D, TOK], BF16, "in"),
          "wo": k.dram("wo", [32, 128, 4096], BF16, "in"), "w1": k.dram("w1", [NFT, 128, 4096], BF16, "in"),
          "w3": k.dram("w3", [NFT, 128, 4096], BF16, "in"), "w2": k.dram("w2", [32, 128, NFT * 128], BF16, "in"),
          "x1T": k.dram("x1T", [D, TOK], F32, "out"), "outT": k.dram("outT", [D, TOK], F32, "out")}
    mod = k.dram("mod", [128, 192], F32, "in")
    io["mt"] = load_mod(k, mod)
    emit_ffn(k, TOK, TB, NFT, io)
    return k.finish()


def emit_ffn(k, TOK, TB, NFT, io):
    xT, wo, w1, w3, w2, x1T, out, mt = (io[n] for n in ("xT", "wo", "w1", "w3", "w2", "x1T", "outT", "mt"))
    mixT = io.get("mixT")
    load_mix = io.get("load_mix")
    ones_mean = make_ones(k, 1.0 / D)
    hT = k.sb("hT", [128, 32, TB], BF16)
    uT = k.sb("uT", [128, NFT, TB], BF16)
    hT2 = uT.view(uT.h[:, 0:32, :]) if (io.get("load_mix") is not None and NFT >= 32) else None
    rstd = k.sb("rstd", [128, TB], F32)
    H1 = (NFT + 1) // 2
    wts = Rot([k.sb("wt", [128, max(4096, H1 * 128)], BF16) for _ in range(3)])
    pss = Rot([k.ps("pg", [128, TB]) for _ in range(6)])
    ps_stat = k.ps("ps_stat", [128, TB])
    sqs = Rot([k.sb("sq", [128, TB], F32) for _ in range(2)])
    tmps = Rot([k.sb("tmp", [128, TB], F32) for _ in range(2)])
    xcs = Rot([k.sb("xc", [128, TB], F32) for _ in range(3)])
    evs = Rot([k.sb("ev", [128, TB], F32) for _ in range(3)])
    mv = mixT.h.rearrange("(kc p) t -> p kc t", p=128) if mixT is not None else None
    x1s = [T(x1T.h, "x1T_%d" % tb, multi=True) for tb in range(TOK // TB)]

    def wtile(src_ap, n_el, src_t=None):
        wt = wts.next()
        k.dma("sp", wt[:, 0:n_el], src_ap, r=[src_t] if src_t is not None else [], w=[wt])
        return wt

    for tb in range(TOK // TB):
        ts = slice(tb * TB, (tb + 1) * TB)
        if load_mix is not None:
            if hT2 is None:
                load_mix(tb, hT)
                hm = hT
            else:
                if tb == 0:
                    hms = [hT, hT2]
                    load_mix(0, hms[0])
                hm = hms[tb % 2]
                if tb + 1 < TOK // TB:
                    load_mix(tb + 1, hms[(tb + 1) % 2])
        else:
            for g in range(4):
                k.dma("sp" if g % 2 == 0 else "pool", hT[:, g * 8:(g + 1) * 8, :], mv[:, g * 8:(g + 1) * 8, ts], w=[hT])
        for nt in range(32):
            wt = wtile(wo.h[nt], 4096, wo)
            ps = pss.next()
            hsrc = hm if load_mix is not None else hT
            for kc in range(32):
                k.mm(ps, ps[:, :], wt, wt[:, kc * 128:(kc + 1) * 128], hsrc, hsrc[:, kc, :], kc == 0, kc == 31)
            xc = xcs.next()
            k.dma(k.stq, xc[:, :], xT.h[nt * 128:(nt + 1) * 128, ts], r=[xT], w=[xc])
            ev = evs.next()
            k.op("dve", lambda e: e.scalar_tensor_tensor(out=ev[:, :], in0=ps[:, :], scalar=mt[:, 64 + nt:65 + nt],
                                                         in1=xc[:, :], op0=ALU.mult, op1=ALU.add),
                 r=[ps, xc, mt], w=[ev])
            k.dma(k.stq, x1T.h[nt * 128:(nt + 1) * 128, ts], ev[:, :], r=[ev], w=[x1s[tb]])
    for tb in range(TOK // TB):
        ts = slice(tb * TB, (tb + 1) * TB)

        def load_chunk(kc):
            xc = xcs.next()
            k.dma(k.stq, xc[:, :], x1T.h[kc * 128:(kc + 1) * 128, ts], r=[x1s[tb]], w=[xc])
            return xc, xc[:, :]

        emit_norm_mod(k, TB, load_chunk, hT,
                      lambda kc: mt[:, 4 * 32 + kc:4 * 32 + kc + 1], lambda kc: mt[:, 3 * 32 + kc:3 * 32 + kc + 1],
                      ones_mean, ps_stat, sqs, tmps, rstd)
        if io.get("hook") is not None:
            io["hook"](tb)
        for ft in range(NFT):
            wa = wtile(w1.h[ft], 4096, w1)
            wb = wtile(w3.h[ft], 4096, w3)
            pa = pss.next()
            pb = pss.next()
            for kc in range(32):
                k.mm(pa, pa[:, :], wa, wa[:, kc * 128:(kc + 1) * 128], hT, hT[:, kc, :], kc == 0, kc == 31)
            for kc in range(32):
                k.mm(pb, pb[:, :], wb, wb[:, kc * 128:(kc + 1) * 128], hT, hT[:, kc, :], kc == 0, kc == 31)
            sa = tmps.next()
            k.op("act", lambda e: e.activation(out=sa[:, :], in_=pa[:, :], func=AF.Silu), r=[pa], w=[sa])
            k.op("dve", lambda e: e.tensor_tensor(out=uT[:, ft, :], in0=sa[:, :], in1=pb[:, :], op=ALU.mult),
                 r=[sa, pb], w=[uT])
        for nt in range(32):
            ps = pss.next()
            for half in range(2):
                f0 = half * H1
                f1 = min(NFT, f0 + H1)
                wt = wtile(w2.h[nt][:, f0 * 128:f1 * 128], (f1 - f0) * 128, w2)
                for fc in range(f0, f1):
                    k.mm(ps, ps[:, :], wt, wt[:, (fc - f0) * 128:(fc - f0 + 1) * 128], uT, uT[:, fc, :],
                         fc == 0, fc == NFT - 1)
            xc, _ = load_chunk(nt)
            ev = evs.next()
            k.op("dve", lambda e: e.scalar_tensor_tensor(out=ev[:, :], in0=ps[:, :], scalar=mt[:, 5 * 32 + nt:5 * 32 + nt + 1],
                                                         in1=xc[:, :], op0=ALU.mult, op1=ALU.add),
                 r=[ps, xc, mt], w=[ev])
            k.dma(k.stq, out.h[nt * 128:(nt + 1) * 128, ts], ev[:, :], r=[ev], w=[out])


NEG = -30000.0


def t5_bucket(dist):
    n = np.maximum(dist, 0)
    nf = np.maximum(n, 1).astype(np.float32)
    large = 16 + (np.log(nf / np.float32(16)) / np.float32(math.log(128 / 16)) * np.float32(16)).astype(np.int32)
    large = np.minimum(large, 31)
    return np.where(n < 16, n, large)


def emit_group_norm_fm(k, src_t, src_ap, dst_t, dst_ap, gm, P, Fshape, gain_ap, gain_t, ps, sqs, tmps, rstds, view):
    sq = sqs.next()
    k.op("act", lambda e: e.activation(out=view(sq), in_=src_ap, func=AF.Square), r=[src_t], w=[sq])
    k.mm(ps, view(ps), gm, gm[0:P, 0:P], sq, view(sq), True, True)
    rstd = rstds.next()
    emit_rsqrt(k, rstd, view(rstd), ps, view(ps), EPS, tmps, shape=view)
    k.op("dve", lambda e: e.scalar_tensor_tensor(out=dst_ap, in0=src_ap, scalar=gain_ap, in1=view(rstd),
                                                 op0=ALU.mult, op1=ALU.mult), r=[src_t, gain_t, rstd], w=[dst_t])


def build_swa(Tn):
    k = K()
    io = {"q4T": k.dram("q4T", [64, 4, Tn], F32, "in"), "kT": k.dram("kT", [64, Tn], F32, "in"),
          "vtok": k.dram("vtok", [Tn, 64], F32, "in"), "par": k.dram("par", [64, 8], F32, "in"),
          "bias": k.dram("bias", [128, 2, 512], F32, "in"), "gm": k.dram("gm", [64, 64], F32, "in"),
          "oT": k.dram("oT", [64, 4, Tn], BF16, "out")}
    emit_swa(k, Tn, io)
    return k.finish()


def emit_swa(k, Tn, io):
    NTT = Tn // 128
    q4, kT, vtok, par, biasd, gmd, out = (io[n] for n in ("q4T", "kT", "vtok", "par", "bias", "gm", "oT"))
    part = k.sb("par", [64, 8], F32)
    k.dma("sp", part[:, :], par[:, :], w=[part])
    bias = k.sb("bias", [128, 2, 512], F32)
    k.dma("sp", bias[:, :, :], biasd[:, :, :], w=[bias])
    gm = k.sb("gm", [64, 64], F32)
    k.dma("sp", gm[:, :], gmd[:, :], w=[gm])
    esink = k.sb("esink", [64, 4], F32)
    k.op("act", lambda e: e.activation(out=esink[:, :], in_=part[:, 2:6], func=AF.Exp), r=[part], w=[esink])
    ones = make_ones(k, 1.0, (128, 64), BF16)
    kn = k.sb("kn", [64, Tn], BF16)
    vb = k.sb("vb", [128, NTT, 64], BF16)
    pss = Rot([k.ps("ps", [128, 512]) for _ in range(8)])
    sqs = Rot([k.sb("sq", [128, 512], F32) for _ in range(2)])
    tmps = Rot([k.sb("tmp", [128, 512], F32) for _ in range(2)])
    rstds = Rot([k.sb("rstd", [128, 512], F32) for _ in range(2)])
    stg = Rot([k.sb("stg", [128, 512], F32) for _ in range(3)])
    v64 = lambda t: t[0:64, :]
    vv = vtok.h.rearrange("(tt p) d -> p tt d", p=128)
    for g in range(0, NTT, 8):
        n = min(8, NTT - g)
        s = stg.next()
        sv = s[:, 0:n * 64].rearrange("p (a d) -> p a d", d=64)
        k.dma("sp", sv, vv[:, g:g + n, :], w=[s])
        k.op("dve", lambda e: e.tensor_copy(out=vb[:, g:g + n, :], in_=sv), r=[s], w=[vb])
    for tb in range(Tn // 512):
        ts = slice(tb * 512, (tb + 1) * 512)
        s = stg.next()
        k.dma("sp", s[0:64, :], kT[:, ts], w=[s])
        emit_group_norm_fm(k, s, s[0:64, :], kn, kn[:, ts], gm, 64, None, part[:, 1:2], part, pss.next(), sqs, tmps, rstds, v64)
    qns = Rot([k.sb("qn", [64, 512], BF16) for _ in range(2)])
    sbs = Rot([k.sb("sb", [128, 512], F32) for _ in range(3)])
    pts = Rot([k.sb("pt", [128, 2, 512], BF16) for _ in range(2)])
    oos = Rot([k.sb("oo", [64, 512], BF16) for _ in range(2)])
    for j in range(NTT):
        s = stg.next()
        sv = s[0:64, :].rearrange("p (h q) -> p h q", q=128)
        k.dma("sp", sv, q4[:, :, j * 128:(j + 1) * 128], w=[s])
        qn = qns.next()
        emit_group_norm_fm(k, s, s[0:64, :], qn, qn[:, :], gm, 64, None, part[:, 0:1], part, pss.next(), sqs, tmps, rstds, v64)
        pt = pts.next()
        which = [1] if j == 0 else [0, 1]
        for w_ in which:
            kt = j - 1 + w_
            ps = pss.next()
            k.mm(ps, ps[:, :], kn, kn[:, kt * 128:(kt + 1) * 128], qn, qn[:, :], True, True)
            sb_ = sbs.next()
            k.op("dve", lambda e: e.scalar_tensor_tensor(out=sb_[:, :], in0=ps[:, :], scalar=0.125, in1=bias[:, w_, :],
                                                         op0=ALU.mult, op1=ALU.add), r=[ps, bias], w=[sb_])
            k.op("act", lambda e: e.activation(out=pt[:, w_, :], in_=sb_[:, :], func=AF.Exp), r=[sb_], w=[pt])
        po = pss.next()
        pd = pss.next()
        for i, w_ in enumerate(which):
            kt = j - 1 + w_
            k.mm(po, po[0:64, :], vb, vb[:, kt, :], pt, pt[:, w_, :], i == 0, i == len(which) - 1)
        for i, w_ in enumerate(which):
            k.mm(pd, pd[0:64, :], ones, ones[:, :], pt, pt[:, w_, :], i == 0, i == len(which) - 1)
        den = tmps.next()
        for h in range(4):
            hs = slice(h * 128, (h + 1) * 128)
            k.op("dve", lambda e: e.tensor_scalar_add(out=den[0:64, hs], in0=pd[0:64, hs], scalar1=esink[:, h:h + 1]),
                 r=[pd, esink], w=[den])
        k.op("dve", lambda e: e.reciprocal(out=den[0:64, :], in_=den[0:64, :]), r=[den], w=[den])
        oo = oos.next()
        k.op("dve", lambda e: e.tensor_tensor(out=oo[:, :], in0=po[0:64, :], in1=den[0:64, :], op=ALU.mult), r=[po, den], w=[oo])
        k.dma(k.stq, out[:, :, j * 128:(j + 1) * 128], oo[:, :].rearrange("p (h q) -> p h q", q=128), r=[oo])


def swa_consts(bias_d4):
    ki = np.arange(128)[:, None]
    qi = np.arange(128)[None, :]
    outb = np.zeros((128, 2, 4, 128), np.float32)
    for w_, off in ((0, 128), (1, 0)):
        dist = off + qi - ki
        valid = (dist >= 0) & (dist < 128)
        b = bias_d4[t5_bucket(dist)]
        outb[:, w_] = np.where(valid[:, None, :], b.transpose(0, 2, 1), np.float32(NEG))
    gm = np.full((64, 64), 1.0 / 64, np.float32)
    return outb.reshape(128, 2, 512), gm


def build_diff(Tn, NH, layer_idx):
    k = K()
    io = {"qT": k.dram("qT", [NH * 128, Tn], F32, "in"), "kT": k.dram("kT", [NH * 128, Tn], F32, "in"),
          "vtok": [k.dram("vtok%d" % h, [Tn, 128], F32, "in") for h in range(NH)],
          "par": k.dram("par", [128, 4 + NH], F32, "in"), "lam": k.dram("lam", [128, 256], F32, "in"),
          "bias": k.dram("bias", [NH, 128, 5, 512], F32, "in"), "gm": k.dram("gm", [128, 128], F32, "in"),
          "oT": k.dram("oT", [NH * 128, Tn], BF16, "out")}
    emit_diff(k, Tn, NH, layer_idx, io)
    return k.finish()


def emit_diff(k, Tn, NH, layer_idx, io):
    NTT = Tn // 128
    NQB = Tn // 512
    lam_init = 0.8 - 0.6 * math.exp(-0.3 * layer_idx)
    qT, kT, vtok, par, lamd, biasd, gmd, out = (io[n] for n in ("qT", "kT", "vtok", "par", "lam", "bias", "gm", "oT"))
    part = k.sb("par", [128, 4 + NH], F32)
    k.dma("sp", part[:, :], par[:, :], w=[part])
    lam = k.sb("lam", [128, 256], F32)
    k.dma("sp", lam[:, :], lamd[:, :], w=[lam])
    gm = k.sb("gm", [128, 128], F32)
    k.dma("sp", gm[:, :], gmd[:, :], w=[gm])
    ones_m = make_ones(k, 1.0 / 128)
    ones_b = make_ones(k, 1.0, (128, 128), BF16)
    lt = k.sb("lt", [128, 128], F32)
    ls = k.sb("ls", [128, 4], F32)
    k.op("dve", lambda e: e.tensor_tensor(out=lt[:, 0:64], in0=lam[:, 0:64], in1=lam[:, 64:128], op=ALU.mult), r=[lam], w=[lt])
    k.op("dve", lambda e: e.tensor_tensor(out=lt[:, 64:128], in0=lam[:, 128:192], in1=lam[:, 192:256], op=ALU.mult), r=[lam], w=[lt])
    k.op("dve", lambda e: e.reduce_sum(out=ls[:, 0:1], in_=lt[:, 0:64], axis=AX.X), r=[lt], w=[ls])
    k.op("dve", lambda e: e.reduce_sum(out=ls[:, 1:2], in_=lt[:, 64:128], axis=AX.X), r=[lt], w=[ls])
    k.op("act", lambda e: e.activation(out=ls[:, 0:2], in_=ls[:, 0:2], func=AF.Exp), r=[ls], w=[ls])
    k.op("dve", lambda e: e.tensor_tensor(out=ls[:, 2:3], in0=ls[:, 1:2], in1=ls[:, 0:1], op=ALU.subtract), r=[ls], w=[ls])
    k.op("dve", lambda e: e.tensor_scalar_add(out=ls[:, 2:3], in0=ls[:, 2:3], scalar1=-lam_init), r=[ls], w=[ls])
    k.op("dve", lambda e: e.tensor_scalar_mul(out=ls[:, 3:4], in0=part[:, 2:3], scalar1=1.0 - lam_init), r=[part], w=[ls])
    qn = k.sb("qn", [128, Tn], BF16)
    kn = k.sb("kn", [128, Tn], BF16)
    vb = k.sb("vb", [128, NTT, 128], BF16)
    bias = k.sb("bias", [128, 5, 512], F32)
    pss = Rot([k.ps("ps", [128, 512]) for _ in range(4)])
    pacc = [k.ps("pacc", [128, 512]) for _ in range(4)]
    sqs = Rot([k.sb("sq", [128, 512], F32) for _ in range(2)])
    tmps = Rot([k.sb("tmp", [128, 512], F32) for _ in range(3)])
    rstds = Rot([k.sb("rstd", [128, 512], F32) for _ in range(2)])
    stg = Rot([k.sb("stg", [128, 512], F32) for _ in range(3)])
    sbs = Rot([k.sb("sb", [128, 512], F32) for _ in range(3)])
    pts = Rot([k.sb("pt", [128, 512], BF16) for _ in range(6)])
    oos = Rot([k.sb("oo", [128, 512], BF16) for _ in range(2)])
    full = lambda t: t[:, :]
    for h in range(NH):
        hr = slice(h * 128, (h + 1) * 128)
        k.dma("sp", bias[:, :, :], biasd.h[h], w=[bias])
        vv = vtok[h].h.rearrange("(tt p) d -> p tt d", p=128)
        for g in range(0, NTT, 4):
            n = min(4, NTT - g)
            s = stg.next()
            sv = s[:, 0:n * 128].rearrange("p (a d) -> p a d", d=128)
            k.dma("sp", sv, vv[:, g:g + n, :], r=[vtok[h]], w=[s])
            k.op("dve", lambda e: e.tensor_copy(out=vb[:, g:g + n, :], in_=sv), r=[s], w=[vb])
        for src, dst, gc in ((qT, qn, 0), (kT, kn, 1)):
            for tb in range(NQB):
                ts = slice(tb * 512, (tb + 1) * 512)
                s = stg.next()
                k.dma("sp", s[:, :], src.h[hr, ts], w=[s])
                emit_group_norm_fm(k, s, s[:, :], dst, dst[:, ts], gm, 128, None, part[:, gc:gc + 1], part, pss.next(),
                                   sqs, tmps, rstds, full)
        for qb in range(NQB):
            qs = slice(qb * 512, (qb + 1) * 512)
            nkt = qb * 4 + 4

            def emit_s(kt):
                res = []
                for p in range(2):
                    ps = pss.next()
                    pr = slice(p * 64, (p + 1) * 64)
                    k.mm(ps, ps[:, :], kn, kn[pr, kt * 128:(kt + 1) * 128], qn, qn[pr, qs], True, True)
                    res.append(ps)
                return res

            cur = emit_s(0)
            for kt in range(nkt):
                nxt = emit_s(kt + 1) if kt + 1 < nkt else None
                far = kt * 128 + 127 + 113 <= qb * 512
                for p in range(2):
                    ps = cur[p]
                    pt = pts.next()
                    if far:
                        k.op("act", lambda e: e.activation(out=pt[:, :], in_=ps[:, :], func=AF.Exp,
                                                           bias=part[:, 4 + h:5 + h], scale=0.125), r=[ps, part], w=[pt])
                    else:
                        jrel = kt - qb * 4 + 1
                        sb_ = sbs.next()
                        k.op("dve", lambda e: e.scalar_tensor_tensor(out=sb_[:, :], in0=ps[:, :], scalar=0.125,
                                                                     in1=bias[:, jrel, :], op0=ALU.mult, op1=ALU.add),
                             r=[ps, bias], w=[sb_])
                        k.op("act", lambda e: e.activation(out=pt[:, :], in_=sb_[:, :], func=AF.Exp), r=[sb_], w=[pt])
                    k.mm(pacc[2 * p], pacc[2 * p][:, :], vb, vb[:, kt, :], pt, pt[:, :], kt == 0, kt == nkt - 1)
                    k.mm(pacc[2 * p + 1], pacc[2 * p + 1][:, :], ones_b, ones_b[:, :], pt, pt[:, :], kt == 0, kt == nkt - 1)
                cur = nxt
            a = []
            for p in range(2):
                r_ = tmps.next()
                k.op("dve", lambda e: e.reciprocal(out=r_[:, :], in_=pacc[2 * p + 1][:, :]), r=[pacc[2 * p + 1]], w=[r_])
                k.op("dve", lambda e: e.tensor_tensor(out=r_[:, :], in0=pacc[2 * p][:, :], in1=r_[:, :], op=ALU.mult),
                     r=[pacc[2 * p], r_], w=[r_])
                a.append(r_)
            o = tmps.next()
            k.op("dve", lambda e: e.scalar_tensor_tensor(out=o[:, :], in0=a[1][:, :], scalar=ls[:, 2:3], in1=a[0][:, :],
                                                         op0=ALU.mult, op1=ALU.add), r=[a[0], a[1], ls], w=[o])
            oo = oos.next()
            emit_group_norm_fm(k, o, o[:, :], oo, oo[:, :], ones_m, 128, None, ls[:, 3:4], ls, pss.next(), sqs, stg, rstds, full)
            k.dma(k.stq, out[hr, qs], oo[:, :], r=[oo])


def diff_consts(bias_a_h):
    NH = bias_a_h.shape[1]
    ki = np.arange(128)[:, None]
    qi = np.arange(512)[None, :]
    outb = np.zeros((NH, 128, 5, 512), np.float32)
    for jj in range(5):
        dist = qi - (ki + (jj - 1) * 128)
        b = bias_a_h[t5_bucket(dist)]
        outb[:, :, jj, :] = np.where((dist >= 0)[None], b.transpose(2, 0, 1), np.float32(NEG))
    gm = np.zeros((128, 128), np.float32)
    gm[0:64, 0:64] = 1.0 / 64
    gm[64:, 64:] = 1.0 / 64
    return outb, bias_a_h[31], gm


def build_gla(Tn):
    k = K()
    io = {"qT": k.dram("qT", [128, Tn], F32, "in"), "kT": k.dram("kT", [128, Tn], F32, "in"),
          "ktok": k.dram("ktok", [Tn, 128], F32, "in"), "vtok": k.dram("vtok", [Tn, 256], F32, "in"),
          "gtok": k.dram("gtok", [Tn, 256], F32, "in"), "gdT": k.dram("gdT", [16, Tn], F32, "in"),
          "gup": k.dram("gup", [16, 128], F32, "in"), "cst": k.dram("cst", [128, 768], F32, "in"),
          "otok": k.dram("otok", [Tn, 256], BF16, "out")}
    emit_gla(k, Tn, io)
    return k.finish()


def emit_fm_store(k, pss, res_t, res_ap, ntok, identb, out_fm, cols, evs):
    ps = pss.next()
    for hf in range(2):
        k.mm(ps, ps[0:128, hf * ntok:(hf + 1) * ntok], res_t, res_ap[:, hf * 128:(hf + 1) * 128], identb, identb[0:ntok, 0:ntok], True, True)
    ev = evs.next()
    k.op("act", lambda e: e.activation(out=ev[:, 0:2 * ntok], in_=ps[0:128, 0:2 * ntok], func=AF.Copy), r=[ps], w=[ev])
    for hf in range(2):
        k.dma(k.stq, out_fm[hf * 128:(hf + 1) * 128, cols], ev[:, hf * ntok:(hf + 1) * ntok], r=[ev], w=[out_fm])


def make_identb(k, n=128):
    raise NotImplementedError


def emit_gla(k, Tn, io):
    NCH = Tn // 128
    qT, kT, ktok, vtok, gtok, gdT, gup, cst = (io[n] for n in ("qT", "kT", "ktok", "vtok", "gtok", "gdT", "gup", "cst"))
    out = io.get("otok")
    out_fm = io.get("oT_fm")
    if out_fm is not None:
        identb = k.sb("identb", [128, 128], BF16)
        identf = k.sb("identf", [128, 128], F32)
        k.dma("sp", identf[:, :], io["ident"][:, :], r=[io["ident"]], w=[identf])
        k.op("dve", lambda e: e.tensor_copy(out=identb[:, :], in_=identf[:, :]), r=[identf], w=[identb])
        fevs = Rot([k.sb("fev", [128, 256], BF16) for _ in range(2)])
    c = k.sb("cst", [128, 768], F32)
    k.dma("sp", c[:, :], cst[:, :], w=[c])
    biasb, tri, sup, maskI, gainb = c[:, 0:128], c[:, 128:256], c[:, 256:384], c[:, 384:512], c[:, 512:768]
    gu = k.sb("gu", [16, 128], F32)
    k.dma("sp", gu[:, :], gup[:, :], w=[gu])
    gd = k.sb("gd", [16, Tn], F32)
    k.dma("sp", gd[:, :], gdT[:, :], r=[gdT], w=[gd])
    H = k.sb("H", [128, 256], F32)
    Hb = k.sb("Hb", [128, 256], BF16)
    k.op(k.aux, lambda e: e.memset(H[:, :], 0.0), w=[H])
    k.op(k.aux, lambda e: e.memset(Hb[:, :], 0.0), w=[Hb])
    pss = Rot([k.ps("ps", [128, 512]) for _ in range(8)])
    R = lambda nm, shp, dt, n=2: Rot([k.sb(nm, shp, dt) for _ in range(n)])
    qs_, ks_, kts_, vs_, gs_ = R("q", [128, 128], F32), R("k", [128, 128], F32), R("kt", [128, 128], F32), R("v", [128, 256], F32), R("g", [128, 256], F32)
    zbs, las = R("zb", [128, 128], F32), R("la", [128, 128], F32)
    ebs, enbs, edts = R("eb", [128, 128], F32), R("enb", [128, 128], F32), R("edt", [128, 128], F32)
    qds, ktls, kds, ats, vbs = R("qd", [128, 128], BF16), R("ktl", [128, 128], BF16), R("kd", [128, 128], BF16), R("at", [128, 128], BF16), R("vb", [128, 256], BF16)
    sml = R("sml", [128, 4], F32)
    sq_, on_, sg_, res_ = R("sqo", [128, 256], F32), R("on", [128, 256], F32), R("sg", [128, 256], F32), R("res", [128, 256], BF16)
    for ch in range(NCH):
        ts = slice(ch * 128, (ch + 1) * 128)
        q, kf, kt_, v, g = qs_.next(), ks_.next(), kts_.next(), vs_.next(), gs_.next()
        k.dma("sp", q[:, :], qT[:, ts], r=[qT], w=[q])
        k.dma("sp", kf[:, :], kT[:, ts], r=[kT], w=[kf])
        k.dma("sp", kt_[:, :], ktok[ts, :], r=[ktok], w=[kt_])
        k.dma("sp", v[:, :], vtok[ts, :], r=[vtok], w=[v])
        k.dma("sp", g[:, :], gtok[ts, :], r=[gtok], w=[g])
        vb = vbs.next()
        k.op(k.aux, lambda e: e.tensor_copy(out=vb[:, :], in_=v[:, :]), r=[v], w=[vb])
        pz = pss.next()
        k.mm(pz, pz[:, 0:128], gd, gd[:, ts], gu, gu[:, :], True, True)
        zb = zbs.next()
        k.op("dve", lambda e: e.tensor_tensor(out=zb[:, :], in0=pz[:, 0:128], in1=biasb, op=ALU.add), r=[pz, c], w=[zb])
        la = las.next()
        k.op("act", lambda e: e.activation(out=zb[:, :], in_=zb[:, :], func=AF.Exp, scale=-1.0), r=[zb], w=[zb])
        k.op("act", lambda e: e.activation(out=la[:, :], in_=zb[:, :], func=AF.Ln, bias=eps_ap(k, 1.0, zb[:, :]), scale=1.0), r=[zb], w=[la])
        pb = pss.next()
        k.mm(pb, pb[:, 0:128], la, la[:, :], c, tri, True, True)
        k.mm(pb, pb[:, 128:256], c, sup, la, la[:, :], True, True)
        eb, enb, edt = ebs.next(), enbs.next(), edts.next()
        k.op("act", lambda e: e.activation(out=eb[:, :], in_=pb[:, 0:128], func=AF.Exp), r=[pb], w=[eb])
        k.op("act", lambda e: e.activation(out=enb[:, :], in_=pb[:, 0:128], func=AF.Exp, scale=-1.0), r=[pb], w=[enb])
        k.op("act", lambda e: e.activation(out=edt[:, :], in_=pb[:, 128:256], func=AF.Exp), r=[pb], w=[edt])
        qd, ktl, kd = qds.next(), ktls.next(), kds.next()
        k.op("dve", lambda e: e.scalar_tensor_tensor(out=qd[:, :], in0=q[:, :], scalar=128 ** -0.5, in1=eb[:, :],
                                                     op0=ALU.mult, op1=ALU.mult), r=[q, eb], w=[qd])
        k.op("dve", lambda e: e.tensor_tensor(out=ktl[:, :], in0=kf[:, :], in1=enb[:, :], op=ALU.mult), r=[kf, enb], w=[ktl])
        k.op("dve", lambda e: e.tensor_tensor(out=kd[:, :], in0=kt_[:, :], in1=edt[:, :], op=ALU.mult), r=[kt_, edt], w=[kd])
        pa = pss.next()
        k.mm(pa, pa[:, 0:128], ktl, ktl[:, :], qd, qd[:, :], True, True)
        at = ats.next()
        k.op("dve", lambda e: e.tensor_tensor(out=at[:, :], in0=pa[:, 0:128], in1=maskI, op=ALU.mult), r=[pa, c], w=[at])
        po = pss.next()
        k.mm(po, po[:, 0:256], at, at[:, :], vb, vb[:, :], True, False)
        k.mm(po, po[:, 0:256], qd, qd[:, :], Hb, Hb[:, :], False, True)
        pu = pss.next()
        k.mm(pu, pu[:, 0:256], kd, kd[:, :], vb, vb[:, :], True, True)
        k.op("dve", lambda e: e.scalar_tensor_tensor(out=H[:, :], in0=H[:, :], scalar=eb[:, 127:128], in1=pu[:, 0:256],
                                                     op0=ALU.mult, op1=ALU.add), r=[H, eb, pu], w=[H])
        k.op("act", lambda e: e.activation(out=Hb[:, :], in_=H[:, :], func=AF.Copy), r=[H], w=[Hb])
        sm = sml.next()
        sq = sq_.next()
        k.op("act", lambda e: e.activation(out=sq[:, :], in_=po[:, 0:256], func=AF.Square, accum_out=sm[:, 0:1]), r=[po], w=[sq, sm])
        k.op("act", lambda e: e.activation(out=sm[:, 1:2], in_=sm[:, 0:1], func=AF.Sqrt, bias=eps_ap(k, EPS, sm[:, 0:1]),
                                           scale=1.0 / 256), r=[sm], w=[sm])
        k.op("dve", lambda e: e.reciprocal(out=sm[:, 2:3], in_=sm[:, 1:2]), r=[sm], w=[sm])
        on = on_.next()
        k.op("dve", lambda e: e.scalar_tensor_tensor(out=on[:, :], in0=po[:, 0:256], scalar=sm[:, 2:3], in1=gainb,
                                                     op0=ALU.mult, op1=ALU.mult), r=[po, sm, c], w=[on])
        sg = sg_.next()
        k.op("act", lambda e: e.activation(out=sg[:, :], in_=g[:, :], func=AF.Silu), r=[g], w=[sg])
        rs = res_.next()
        k.op("dve", lambda e: e.tensor_tensor(out=rs[:, :], in0=on[:, :], in1=sg[:, :], op=ALU.mult), r=[on, sg], w=[rs])
        if out_fm is not None:
            emit_fm_store(k, pss, rs, rs, 128, identb, out_fm, ts, fevs)
        else:
            k.dma(k.stq, out[ts, :], rs[:, :], r=[rs])


def gla_consts(gate_bias_h, out_gain):
    s = np.arange(128)[:, None]
    t = np.arange(128)[None, :]
    cst = np.zeros((128, 768), np.float32)
    cst[:, 0:128] = gate_bias_h[None, :]
    cst[:, 128:256] = np.where(s <= t, -1.0 / 16, 0.0)
    cst[:, 256:384] = np.where(s > t, -1.0 / 16, 0.0)
    cst[:, 384:512] = np.where(s <= t, 1.0, 0.0)
    cst[:, 512:768] = out_gain[None, :]
    return cst


RW_DEC = -math.exp(-0.5)
LN_EPS = 64e-5


def build_rwkv(Tn, layer1):
    k = K()
    io = {"rk": k.dram("rk", [64, 8, Tn + 1], F32, "in"), "lo": k.dram("lo", [64, 3, Tn + 1], F32, "in"),
          "vt": k.dram("vt", [Tn + 1, 256], F32, "in"), "pfm": k.dram("pfm", [64, 512 + 192 + 4 * 256], F32, "in"),
          "ptm": k.dram("ptm", [64, 5 * 256], F32, "in"), "mats": k.dram("mats", [64, 768], F32, "in"),
          "cst": k.dram("cst", [64, 1280], F32, "in"), "otok": k.dram("otok", [Tn, 256], BF16, "out")}
    if layer1:
        io.update({"xv": k.dram("xv", [32, Tn + 1], F32, "in"), "vf": k.dram("vf", [Tn, 256], F32, "in"),
                   "vmat": k.dram("vmat", [32, 320], F32, "in")})
    else:
        io["vfirst"] = k.dram("vfirst", [Tn, 256], F32, "out")
    emit_rwkv(k, Tn, layer1, io)
    return k.finish()


def emit_rwkv(k, Tn, layer1, io):
    NCH = Tn // 64
    rk, lo, vt, pfm, ptm, mats, cst = (io[n] for n in ("rk", "lo", "vt", "pfm", "ptm", "mats", "cst"))
    out = io.get("otok")
    out_fm = io.get("oT_fm")
    if out_fm is not None:
        identb = k.sb("identb", [64, 64], BF16)
        fevs = Rot([k.sb("fev", [128, 128], BF16) for _ in range(2)])
    if layer1:
        xv, vf, vmat = io["xv"], io["vf"], io["vmat"]
        vm = k.sb("vm", [32, 320], F32)
        k.dma("sp", vm[:, :], vmat[:, :], r=[vmat], w=[vm])
    else:
        vfo = io["vfirst"]
    pf = k.sb("pf", [64, 1728], F32)
    pt_ = k.sb("ptm", [64, 1280], F32)
    mt = k.sb("mats", [64, 768], F32)
    cs = k.sb("cst", [64, 1280], F32)
    for t_, d_ in ((pf, pfm), (pt_, ptm), (mt, mats), (cs, cst)):
        k.dma("sp", t_[:, :], d_[:, :], r=[d_], w=[t_])
    if out_fm is not None:
        k.op("dve", lambda e: e.tensor_copy(out=identb[:, :], in_=cs[:, 1152:1216]), r=[cs], w=[identb])
    v3 = lambda ap, b: ap.rearrange("p (a b) -> p a b", b=b)
    mu_rk = v3(pf[:, 0:512], 64)
    mu_lo = v3(pf[:, 512:704], 64)
    a0f, kkf, kaf, rkf = (pf[:, 704 + i * 256:704 + (i + 1) * 256] for i in range(4))
    muv, w0b, lnw, lnb, v0b = (pt_[:, i * 256:(i + 1) * 256] for i in range(5))
    w_up, a_up, g_up = mt[:, 0:256], mt[:, 256:512], mt[:, 512:768]
    triIS = cs[:, 0:128]
    mask1 = cs[:, 128:640]
    mL = cs[:, 640:896]
    identF = cs[:, 896:1152]
    ident = cs[:, 1152:1216]
    ones = cs[:, 1216:1280]
    H = k.sb("H", [64, 256], F32)
    k.op(k.aux, lambda e: e.memset(H[:, :], 0.0), w=[H])
    pss = Rot([k.ps("ps", [64, 512]) for _ in range(7 if out_fm is not None else 8)])
    if out_fm is not None:
        pfm_ps = Rot([k.ps("psfm", [128, 512])])
    R = lambda nm, w_, n=2, dt=F32: Rot([k.sb(nm, [64, w_], dt) for _ in range(n)])
    rkts, lots, vps, vcs = R("rkt", 8 * 65), R("lot", 3 * 65), R("vp", 256), R("vc", 256)
    ds_, xss, lds, loss = R("d", 512), R("xs", 512), R("ld", 192), R("los", 192)
    t256 = R("t256", 256, 40)
    t512 = R("t512", 512, 8)
    sm = R("sm", 16, 3)
    ress = R("res", 256, 2, BF16)
    if layer1:
        xvts, vfts = Rot([k.sb("xvt", [32, 65], F32) for _ in range(2)]), R("vft", 256)
        xvss = Rot([k.sb("xvs", [32, 64], F32) for _ in range(2)])
    hsl = lambda h, w_=64: slice(h * w_, (h + 1) * w_)

    def dve(fn, r, w):
        k.op("dve", fn, r=r, w=w)

    def act(fn, r, w):
        k.op("act", fn, r=r, w=w)

    for ch in range(NCH):
        c0 = ch * 64
        rkt, lot, vp, vc = rkts.next(), lots.next(), vps.next(), vcs.next()
        rk3 = v3(rkt[:, :], 65)
        lo3 = v3(lot[:, :], 65)
        k.dma("sp", rk3, rk[:, :, c0:c0 + 65], r=[rk], w=[rkt])
        k.dma("sp", lo3, lo[:, :, c0:c0 + 65], r=[lo], w=[lot])
        k.dma("sp", vp[:, :], vt[c0:c0 + 64, :], r=[vt], w=[vp])
        k.dma("sp", vc[:, :], vt[c0 + 1:c0 + 65, :], r=[vt], w=[vc])
        d, xs = ds_.next(), xss.next()
        d3, xs3 = v3(d[:, :], 64), v3(xs[:, :], 64)
        dve(lambda e: e.tensor_tensor(out=d3, in0=rk3[:, :, 0:64], in1=rk3[:, :, 1:65], op=ALU.subtract), [rkt], [d])
        dve(lambda e: e.tensor_tensor(out=d3, in0=d3, in1=mu_rk, op=ALU.mult), [d, pf], [d])
        dve(lambda e: e.tensor_tensor(out=xs3, in0=d3, in1=rk3[:, :, 1:65], op=ALU.add), [d, rkt], [xs])
        rs, ks = xs[:, 0:256], xs[:, 256:512]
        ld, los = lds.next(), loss.next()
        ld3, los3 = v3(ld[:, :], 64), v3(los[:, :], 64)
        dve(lambda e: e.tensor_tensor(out=ld3, in0=lo3[:, :, 0:64], in1=lo3[:, :, 1:65], op=ALU.subtract), [lot], [ld])
        dve(lambda e: e.tensor_tensor(out=ld3, in0=ld3, in1=mu_lo, op=ALU.mult), [ld, pf], [ld])
        dve(lambda e: e.tensor_tensor(out=los3, in0=ld3, in1=lo3[:, :, 1:65], op=ALU.add), [ld, lot], [los])
        vs = t256.next()
        dve(lambda e: e.tensor_tensor(out=vs[:, :], in0=vp[:, :], in1=vc[:, :], op=ALU.subtract), [vp, vc], [vs])
        dve(lambda e: e.tensor_tensor(out=vs[:, :], in0=vs[:, :], in1=muv, op=ALU.mult), [vs, pt_], [vs])
        dve(lambda e: e.tensor_tensor(out=vs[:, :], in0=vs[:, :], in1=vc[:, :], op=ALU.add), [vs, vc], [vs])
        tw, sgd = t256.next(), t256.next()
        act(lambda e: e.activation(out=tw[:, 0:64], in_=los[:, 0:64], func=AF.Tanh), [los], [tw])
        act(lambda e: e.activation(out=sgd[:, 0:64], in_=los[:, 128:192], func=AF.Sigmoid), [los], [sgd])
        pz = pss.next()
        k.mm(pz, pz[:, 0:256], tw, tw[:, 0:64], mt, w_up, True, True)
        law = t256.next()
        dve(lambda e: e.tensor_tensor(out=law[:, :], in0=pz[:, 0:256], in1=w0b, op=ALU.add), [pz, pt_], [law])
        act(lambda e: e.activation(out=law[:, :], in_=law[:, :], func=AF.Sigmoid), [law], [law])
        pa = pss.next()
        for h in range(4):
            k.mm(pa, pa[:, hsl(h)], mt, a_up[:, hsl(h)], los, los[:, 64:128], True, True)
        a = t256.next()
        dve(lambda e: e.tensor_tensor(out=a[:, :], in0=pa[:, 0:256], in1=a0f, op=ALU.add), [pa, pf], [a])
        act(lambda e: e.activation(out=a[:, :], in_=a[:, :], func=AF.Sigmoid), [a], [a])
        pg = pss.next()
        k.mm(pg, pg[:, 0:256], sgd, sgd[:, 0:64], mt, g_up, True, True)
        gt = t256.next()
        act(lambda e: e.activation(out=gt[:, :], in_=pg[:, 0:256], func=AF.Copy), [pg], [gt])
        if layer1:
            xvt, vft, xvs = xvts.next(), vfts.next(), xvss.next()
            k.dma("sp", xvt[:, :], xv[:, c0:c0 + 65], r=[xv], w=[xvt])
            k.dma("sp", vft[:, :], vf[c0:c0 + 64, :], r=[vf], w=[vft])
            dve(lambda e: e.tensor_tensor(out=xvs[:, :], in0=xvt[:, 0:64], in1=xvt[:, 1:65], op=ALU.subtract), [xvt], [xvs])
            dve(lambda e: e.tensor_tensor(out=xvs[:, :], in0=xvs[:, :], in1=vm[:, 256:320], op=ALU.mult), [xvs, vm], [xvs])
            dve(lambda e: e.tensor_tensor(out=xvs[:, :], in0=xvs[:, :], in1=xvt[:, 1:65], op=ALU.add), [xvs, xvt], [xvs])
            pv = pss.next()
            k.mm(pv, pv[:, 0:256], xvs, xvs[:, :], vm, vm[:, 0:256], True, True)
            gate = t256.next()
            dve(lambda e: e.tensor_tensor(out=gate[:, :], in0=pv[:, 0:256], in1=v0b, op=ALU.add), [pv, pt_], [gate])
            act(lambda e: e.activation(out=gate[:, :], in_=gate[:, :], func=AF.Sigmoid), [gate], [gate])
            v = t256.next()
            dve(lambda e: e.tensor_tensor(out=v[:, :], in0=vft[:, :], in1=vs[:, :], op=ALU.subtract), [vft, vs], [v])
            dve(lambda e: e.tensor_tensor(out=v[:, :], in0=v[:, :], in1=gate[:, :], op=ALU.mult), [v, gate], [v])
            dve(lambda e: e.tensor_tensor(out=v[:, :], in0=v[:, :], in1=vs[:, :], op=ALU.add), [v, vs], [v])
        else:
            v = vs
            k.dma(k.stq, vfo[c0:c0 + 64, :], vs[:, :], r=[vs], w=[vfo])
        kk, sqk = t256.next(), t256.next()
        dve(lambda e: e.tensor_tensor(out=kk[:, :], in0=ks, in1=kkf, op=ALU.mult), [xs, pf], [kk])
        act(lambda e: e.activation(out=sqk[:, :], in_=kk[:, :], func=AF.Square), [kk], [sqk])
        pn = pss.next()
        k.mm(pn, pn[:, 0:256], cs, ones, sqk, sqk[:, :], True, True)
        act(lambda e: e.activation(out=sqk[:, :], in_=pn[:, 0:256], func=AF.Sqrt), [pn], [sqk])
        dve(lambda e: e.tensor_scalar_max(out=sqk[:, :], in0=sqk[:, :], scalar1=1e-12), [sqk], [sqk])
        dve(lambda e: e.reciprocal(out=sqk[:, :], in_=sqk[:, :]), [sqk], [sqk])
        dve(lambda e: e.tensor_tensor(out=kk[:, :], in0=kk[:, :], in1=sqk[:, :], op=ALU.mult), [kk, sqk], [kk])
        km = t256.next()
        dve(lambda e: e.tensor_scalar_add(out=km[:, :], in0=a[:, :], scalar1=-1.0), [a], [km])
        dve(lambda e: e.tensor_tensor(out=km[:, :], in0=km[:, :], in1=kaf, op=ALU.mult), [km, pf], [km])
        dve(lambda e: e.scalar_tensor_tensor(out=km[:, :], in0=km[:, :], scalar=1.0, in1=ks, op0=ALU.add, op1=ALU.mult), [km, xs], [km])
        vbv = t256.next()
        dve(lambda e: e.tensor_tensor(out=vbv[:, :], in0=kk[:, :], in1=a[:, :], op=ALU.mult), [kk, a], [vbv])
        rkr = t256.next()
        dve(lambda e: e.tensor_tensor(out=rkr[:, :], in0=rs, in1=km[:, :], op=ALU.mult), [xs, km], [rkr])
        dve(lambda e: e.tensor_tensor(out=rkr[:, :], in0=rkr[:, :], in1=rkf, op=ALU.mult), [rkr, pf], [rkr])
        prk = pss.next()
        for h in range(4):
            k.mm(prk, prk[:, h:h + 1], rkr, rkr[:, hsl(h)], cs, ones[:, 0:1], True, True)
        smt = sm.next()
        dve(lambda e: e.tensor_copy(out=smt[:, 0:4], in_=prk[:, 0:4]), [prk], [smt])
        pcs = pss.next()
        for h in range(4):
            k.mm(pcs, pcs[:, hsl(h, 128)], law, law[:, hsl(h)], cs, triIS, True, True)
        ep, en = t512.next(), t256.next()
        pcs3, ep3, en3 = v3(pcs[:, :], 128), v3(ep[:, :], 128), v3(en[:, :], 64)
        act(lambda e: e.activation(out=ep[:, :], in_=pcs[:, :], func=AF.Exp), [pcs], [ep])
        act(lambda e: e.activation(out=en3, in_=pcs3[:, :, 0:64], func=AF.Exp, scale=-1.0), [pcs], [en])
        AR, Bt, Kt = t512.next(), t256.next(), t256.next()
        AR3 = v3(AR[:, :], 128)
        dve(lambda e: e.scalar_tensor_tensor(out=AR3[:, :, 0:64], in0=v3(kk[:, :], 64), scalar=-1.0, in1=ep3[:, :, 64:128],
                                             op0=ALU.mult, op1=ALU.mult), [kk, ep], [AR])
        dve(lambda e: e.tensor_tensor(out=AR3[:, :, 64:128], in0=v3(rs, 64), in1=ep3[:, :, 0:64], op=ALU.mult), [xs, ep], [AR])
        dve(lambda e: e.tensor_tensor(out=Bt[:, :], in0=vbv[:, :], in1=en[:, :], op=ALU.mult), [vbv, en], [Bt])
        dve(lambda e: e.tensor_tensor(out=Kt[:, :], in0=km[:, :], in1=en[:, :], op=ALU.mult), [km, en], [Kt])
        p1, p2, p3, ptr = pss.next(), pss.next(), pss.next(), pss.next()
        for h in range(4):
            k.mm(p1, p1[:, hsl(h, 128)], Bt, Bt[:, hsl(h)], AR, AR[:, hsl(h, 128)], True, True)
        for h in range(4):
            k.mm(p2, p2[:, hsl(h, 128)], Kt, Kt[:, hsl(h)], AR, AR[:, hsl(h, 128)], True, True)
        for h in range(4):
            k.mm(p3, p3[:, hsl(h)], AR, AR[:, h * 128:h * 128 + 64], Bt, Bt[:, hsl(h)], True, True)
        for h in range(4):
            k.mm(ptr, ptr[:, hsl(h)], Bt, Bt[:, hsl(h)], cs, ident, True, True)
            k.mm(ptr, ptr[:, 256 + h * 64:256 + (h + 1) * 64], Kt, Kt[:, hsl(h)], cs, ident, True, True)
        M1, M2, L, BKT = t512.next(), t512.next(), t256.next(), t512.next()
        dve(lambda e: e.tensor_tensor(out=M1[:, :], in0=p1[:, :], in1=mask1, op=ALU.mult), [p1, cs], [M1])
        dve(lambda e: e.tensor_tensor(out=M2[:, :], in0=p2[:, :], in1=mask1, op=ALU.mult), [p2, cs], [M2])
        dve(lambda e: e.tensor_tensor(out=L[:, :], in0=p3[:, 0:256], in1=mL, op=ALU.mult), [p3, cs], [L])
        act(lambda e: e.activation(out=BKT[:, :], in_=ptr[:, :], func=AF.Copy), [ptr], [BKT])
        Y, YT = t256.next(), t256.next()
        M13 = v3(M1[:, :], 128)
        dve(lambda e: e.tensor_tensor(out=v3(Y[:, :], 64), in0=M13[:, :, 0:64], in1=v3(identF, 64), op=ALU.add), [M1, cs], [Y])
        dve(lambda e: e.tensor_tensor(out=YT[:, :], in0=L[:, :], in1=identF, op=ALU.add), [L, cs], [YT])
        Pt, Pv = M1, (lambda h: M1[:, h * 128:h * 128 + 64])
        PTt, PTv = L, (lambda h: L[:, hsl(h)])
        for lvl in range(5):
            last = lvl == 4
            pa_, pb_ = pss.next(), pss.next()
            for h in range(4):
                k.mm(pa_, pa_[:, hsl(h)], PTt, PTv(h), Pt, Pv(h), True, True)
            if not last:
                for h in range(4):
                    k.mm(pb_, pb_[:, hsl(h)], Pt, Pv(h), PTt, PTv(h), True, True)
            P2 = t256.next()
            dve(lambda e: e.tensor_copy(out=P2[:, :], in_=pa_[:, 0:256]), [pa_], [P2])
            if not last:
                P2T = t256.next()
                act(lambda e: e.activation(out=P2T[:, :], in_=pb_[:, 0:256], func=AF.Copy), [pb_], [P2T])
            pc_, pd_ = pss.next(), pss.next()
            for h in range(4):
                k.mm(pc_, pc_[:, hsl(h)], YT, YT[:, hsl(h)], P2, P2[:, hsl(h)], True, True)
            if not last:
                for h in range(4):
                    k.mm(pd_, pd_[:, hsl(h)], P2, P2[:, hsl(h)], YT, YT[:, hsl(h)], True, True)
            dve(lambda e: e.tensor_tensor(out=Y[:, :], in0=Y[:, :], in1=pc_[:, 0:256], op=ALU.add), [Y, pc_], [Y])
            if not last:
                dve(lambda e: e.tensor_tensor(out=YT[:, :], in0=YT[:, :], in1=pd_[:, 0:256], op=ALU.add), [YT, pd_], [YT])
                Pt, PTt = P2, P2T
                Pv = (lambda h, P2=P2: P2[:, hsl(h)])
                PTv = (lambda h, P2T=P2T: P2T[:, hsl(h)])
        pr = pss.next()
        for h in range(4):
            k.mm(pr, pr[:, hsl(h)], AR, AR[:, h * 128:h * 128 + 64], H, H[:, hsl(h)], True, False)
            k.mm(pr, pr[:, hsl(h)], M2, M2[:, h * 128:h * 128 + 64], v, v[:, hsl(h)], False, True)
        RHS = t256.next()
        dve(lambda e: e.tensor_copy(out=RHS[:, :], in_=pr[:, 0:256]), [pr], [RHS])
        pu = pss.next()
        for h in range(4):
            k.mm(pu, pu[:, hsl(h)], Y, Y[:, hsl(h)], RHS, RHS[:, hsl(h)], True, True)
        U = t256.next()
        dve(lambda e: e.tensor_copy(out=U[:, :], in_=pu[:, 0:256]), [pu], [U])
        py, ph = pss.next(), pss.next()
        for h in range(4):
            k.mm(py, py[:, hsl(h)], AR, AR[:, h * 128 + 64:h * 128 + 128], H, H[:, hsl(h)], True, False)
            k.mm(py, py[:, hsl(h)], M1, M1[:, h * 128 + 64:h * 128 + 128], U, U[:, hsl(h)], False, False)
            k.mm(py, py[:, hsl(h)], M2, M2[:, h * 128 + 64:h * 128 + 128], v, v[:, hsl(h)], False, True)
        for h in range(4):
            k.mm(ph, ph[:, hsl(h)], BKT, BKT[:, hsl(h)], U, U[:, hsl(h)], True, False)
            k.mm(ph, ph[:, hsl(h)], BKT, BKT[:, 256 + h * 64:256 + (h + 1) * 64], v, v[:, hsl(h)], False, True)
        dve(lambda e: e.tensor_tensor(out=H[:, :], in0=H[:, :], in1=ph[:, 0:256], op=ALU.add), [H, ph], [H])
        for h in range(4):
            dve(lambda e: e.tensor_scalar_mul(out=H[:, hsl(h)], in0=H[:, hsl(h)], scalar1=ep[:, h * 128 + 63:h * 128 + 64]), [H, ep], [H])
        ysq, yo = t256.next(), t256.next()
        dve(lambda e: e.reduce_sum(out=smt[:, 4:8], in_=v3(py[:, 0:256], 64), axis=AX.X), [py], [smt])
        act(lambda e: e.activation(out=ysq[:, :], in_=py[:, 0:256], func=AF.Square), [py], [ysq])
        dve(lambda e: e.reduce_sum(out=smt[:, 8:12], in_=v3(ysq[:, :], 64), axis=AX.X), [ysq], [smt])
        dve(lambda e: e.tensor_scalar_mul(out=smt[:, 4:8], in0=smt[:, 4:8], scalar1=1.0 / 64), [smt], [smt])
        dve(lambda e: e.tensor_tensor(out=smt[:, 12:16], in0=smt[:, 4:8], in1=smt[:, 4:8], op=ALU.mult), [smt], [smt])
        dve(lambda e: e.scalar_tensor_tensor(out=smt[:, 8:12], in0=smt[:, 8:12], scalar=1.0 / 64, in1=smt[:, 12:16],
                                             op0=ALU.mult, op1=ALU.subtract), [smt], [smt])
        act(lambda e: e.activation(out=smt[:, 8:12], in_=smt[:, 8:12], func=AF.Sqrt, bias=eps_ap(k, LN_EPS, smt[:, 8:12]), scale=1.0),
            [smt], [smt])
        dve(lambda e: e.reciprocal(out=smt[:, 8:12], in_=smt[:, 8:12]), [smt], [smt])
        for h in range(4):
            dve(lambda e: e.tensor_scalar(out=yo[:, hsl(h)], in0=py[:, hsl(h)], scalar1=smt[:, 4 + h:5 + h], scalar2=smt[:, 8 + h:9 + h],
                                          op0=ALU.subtract, op1=ALU.mult), [py, smt], [yo])
        dve(lambda e: e.tensor_tensor(out=yo[:, :], in0=yo[:, :], in1=lnw, op=ALU.mult), [yo, pt_], [yo])
        dve(lambda e: e.tensor_tensor(out=yo[:, :], in0=yo[:, :], in1=lnb, op=ALU.add), [yo, pt_], [yo])
        for h in range(4):
            dve(lambda e: e.scalar_tensor_tensor(out=yo[:, hsl(h)], in0=v[:, hsl(h)], scalar=smt[:, h:h + 1], in1=yo[:, hsl(h)],
                                                 op0=ALU.mult, op1=ALU.add), [v, smt, yo], [yo])
        rs_ = ress.next()
        dve(lambda e: e.tensor_tensor(out=rs_[:, :], in0=yo[:, :], in1=gt[:, :], op=ALU.mult), [yo, gt], [rs_])
        if out_fm is not None:
            emit_fm_store(k, pfm_ps, rs_, rs_, 64, identb, out_fm, slice(c0, c0 + 64), fevs)
        else:
            k.dma(k.stq, out[c0:c0 + 64, :], rs_[:, :], r=[rs_])


def rwkv_consts():
    s = np.arange(64)[:, None]
    t = np.arange(64)[None, :]
    c = np.zeros((64, 1280), np.float32)
    c[:, 0:64] = np.where(s <= t, RW_DEC, 0.0)
    c[:, 64:128] = np.where(s < t, RW_DEC, 0.0)
    m1 = np.concatenate([np.where(s < t, 1.0, 0.0), np.where(s <= t, 1.0, 0.0)], axis=1)
    c[:, 128:640] = np.tile(m1, (1, 4))
    c[:, 640:896] = np.tile(np.where(t < s, 1.0, 0.0), (1, 4))
    c[:, 896:1152] = np.tile(np.eye(64), (1, 4))
    c[:, 1152:1216] = np.eye(64)
    c[:, 1216:1280] = 1.0
    return c.astype(np.float32)


def rwkv_inputs(colsT_c, ch, p, layer1, xvT=None, vfirst=None):
    Tn = colsT_c.shape[1]
    G = 1024

    def lead0(a):
        return np.concatenate([np.zeros(a.shape[:-1] + (1,), np.float32), a], axis=-1)

    r = colsT_c[0:G][ch].reshape(4, 64, Tn).transpose(1, 0, 2)
    kk = colsT_c[G:2 * G][ch].reshape(4, 64, Tn).transpose(1, 0, 2)
    rk = lead0(np.concatenate([r, kk], axis=1))
    lo = lead0(colsT_c[3 * G:3 * G + 192].reshape(3, 64, Tn).transpose(1, 0, 2))
    vtok = colsT_c[2 * G:3 * G][ch].T
    vt = np.concatenate([np.zeros((1, 256), np.float32), vtok], axis=0)
    mu = p["rwkv_mu"]
    fm = lambda vec: np.broadcast_to(vec.reshape(4, 64).T[:, :, None], (64, 4, 64)).reshape(64, 256)
    mu_rk = np.concatenate([fm(mu[0:G][ch]), fm(mu[G:2 * G][ch])], axis=1)
    mu_lo = np.broadcast_to(mu[3 * G:3 * G + 192].reshape(3, 64).T[:, :, None], (64, 3, 64)).reshape(64, 192)
    pfm = np.concatenate([mu_rk, mu_lo, fm(p["rwkv_a0"][ch]), fm(p["rwkv_k_k"][ch]), fm(p["rwkv_k_a"][ch]),
                          fm(p["rwkv_r_k"].reshape(-1)[ch])], axis=1)
    tmb = lambda vec: np.broadcast_to(vec[None, :], (64, 256))
    v0 = p["rwkv_v0"][ch] if layer1 else np.zeros(256, np.float32)
    ptm = np.concatenate([tmb(mu[2 * G:3 * G][ch]), tmb(p["rwkv_w0"][ch]), tmb(p["rwkv_ln_w"][ch]), tmb(p["rwkv_ln_b"][ch]),
                          tmb(v0)], axis=1)
    mats = np.concatenate([p["rwkv_w_up"][:, ch], p["rwkv_a_up"][:, ch], p["rwkv_g_up"][:, ch]], axis=1)
    d = {"rk": rk, "lo": lo, "vt": vt, "pfm": pfm, "ptm": ptm, "mats": mats, "cst": rwkv_consts()}
    if layer1:
        d["xv"] = lead0(xvT)
        d["vf"] = vfirst
        d["vmat"] = np.concatenate([p["rwkv_vres_up"][:, ch], np.broadcast_to(p["rwkv_vres_mu"][:, None], (32, 64))], axis=1)
    return {a: np.ascontiguousarray(b, dtype=np.float32) for a, b in d.items()}


SEQ = 8192
TOKC = 2048
O_GLA, O_RW, O_SWA, O_XV = 3072, 6160, 9424, 10704
NT_IN = 84
_PROG = {}


def prog(key, fn):
    if key not in _PROG:
        _PROG[key] = fn()
    return _PROG[key]


def bf16_rows_T(a):
    return np.ascontiguousarray(a.T)


def kernel_unfused(x, c, rel_bias, ada_w, ada_b, w_in, w_out, diff_q_norm, diff_k_norm, diff_lambda, diff_subln,
           gla_gate_up, gla_gate_bias, gla_out_norm, rwkv_mu, rwkv_w_up, rwkv_w0, rwkv_a_up, rwkv_a0, rwkv_g_up,
           rwkv_k_k, rwkv_k_a, rwkv_r_k, rwkv_ln_w, rwkv_ln_b, rwkv_vres_down, rwkv_vres_mu, rwkv_vres_up, rwkv_v0,
           swa_q_norm, swa_k_norm, swa_sinks, ffn_w1, ffn_w3, ffn_w2):
    f = lambda a: np.asarray(a, dtype=np.float32)
    x, c, rel_bias = f(x), f(c), f(rel_bias)
    B, T = x.shape[0], x.shape[1]
    mod = run_ada(c, f(ada_w), f(ada_b))
    tiled = []
    for l in range(2):
        wi = np.zeros((D, NT_IN * 128), np.float32)
        wi[:, :10704] = f(w_in[l])
        if l == 1:
            wi[:, O_XV:O_XV + 32] = f(rwkv_vres_down[0])
        tiled += [tile_weight(wi), tile_weight(f(w_out[l])), tile_weight(f(ffn_w1[l])), tile_weight(f(ffn_w3[l])),
                  tile_weight(f(ffn_w2[l]))]
        del wi
    wb = cast_weights(tiled)
    del tiled
    xT = [np.ascontiguousarray(x[b].T) for b in range(B)]
    vfirst = None
    for l in range(2):
        w_in_b, w_out_b, w1_b, w3_b, w2_b = wb[l * 5:(l + 1) * 5]
        mods = [mod_layout(mod[l, b]) for b in range(B)]
        TOKC = T // 4
        nc = prog(("inproj", T), lambda: build_inproj(TOKC, NT_IN))
        maps = []
        for i in range(NCORES):
            b, tq = i // 4, i % 4
            maps.append({"xT": np.ascontiguousarray(xT[b][:, tq * TOKC:(tq + 1) * TOKC]), "mod": mods[b], "w": w_in_b})
        res = run(nc, maps)
        colsT = [np.concatenate([res[b * 4 + tq]["colsT"] for tq in range(4)], axis=1) for b in range(B)]
        del res, maps
        mixT = [np.zeros((D, T), NPBF) for _ in range(B)]
        nc = prog(("diff", l, T), lambda: build_diff(T, 2, l))
        maps = []
        for i in range(NCORES):
            b, q = i // 4, i % 4
            rows = slice(q * 256, (q + 1) * 256)
            bt, far, gm = diff_consts(rel_bias[:, 2 * q:2 * q + 2])
            par = np.zeros((128, 6), np.float32)
            par[:, 0] = np.tile(f(diff_q_norm[l]), 2)
            par[:, 1] = np.tile(f(diff_k_norm[l]), 2)
            par[:, 2] = f(diff_subln[l])
            par[:, 4:6] = far[None, :]
            vt = np.ascontiguousarray(colsT[b][2048:3072][rows].reshape(2, 128, T).transpose(0, 2, 1))
            maps.append({"qT": np.ascontiguousarray(colsT[b][0:1024][rows]), "kT": np.ascontiguousarray(colsT[b][1024:2048][rows]),
                         "vtok0": vt[0], "vtok1": vt[1], "par": par,
                         "lam": np.ascontiguousarray(np.broadcast_to(f(diff_lambda[l]).reshape(1, 256), (128, 256))),
                         "bias": bt, "gm": gm})
        res = run(nc, maps)
        for i in range(NCORES):
            b, q = i // 4, i % 4
            mixT[b][q * 256:(q + 1) * 256] = res[i]["oT"]
        nc = prog(("gla", T), lambda: build_gla(T))
        maps = []
        for i in range(NCORES):
            b, q = i // 4, i % 4
            cg = colsT[b][O_GLA:O_RW]
            kf = cg[512 + q * 128:512 + (q + 1) * 128]
            maps.append({"qT": np.ascontiguousarray(cg[q * 128:(q + 1) * 128]), "kT": np.ascontiguousarray(kf),
                         "ktok": np.ascontiguousarray(kf.T),
                         "vtok": np.ascontiguousarray(cg[1024 + q * 256:1024 + (q + 1) * 256].T),
                         "gtok": np.ascontiguousarray(cg[2048 + q * 256:2048 + (q + 1) * 256].T),
                         "gdT": np.ascontiguousarray(cg[3072:3088]),
                         "gup": np.ascontiguousarray(f(gla_gate_up[l])[:, q * 128:(q + 1) * 128]),
                         "cst": gla_consts(f(gla_gate_bias[l])[q * 128:(q + 1) * 128], f(gla_out_norm[l]))})
        res = run(nc, maps)
        for i in range(NCORES):
            b, q = i // 4, i % 4
            mixT[b][1024 + q * 256:1024 + (q + 1) * 256] = np.asarray(res[i]["otok"]).T
        nc = prog(("rwkv", l, T), lambda: build_rwkv(T, l == 1))
        p = {"rwkv_mu": f(rwkv_mu[l]), "rwkv_w_up": f(rwkv_w_up[l]), "rwkv_w0": f(rwkv_w0[l]), "rwkv_a_up": f(rwkv_a_up[l]),
             "rwkv_a0": f(rwkv_a0[l]), "rwkv_g_up": f(rwkv_g_up[l]), "rwkv_k_k": f(rwkv_k_k[l]), "rwkv_k_a": f(rwkv_k_a[l]),
             "rwkv_r_k": f(rwkv_r_k[l]), "rwkv_ln_w": f(rwkv_ln_w[l]), "rwkv_ln_b": f(rwkv_ln_b[l])}
        if l == 1:
            p.update({"rwkv_vres_mu": f(rwkv_vres_mu[0]), "rwkv_vres_up": f(rwkv_vres_up[0]), "rwkv_v0": f(rwkv_v0[0])})
        maps = []
        for i in range(NCORES):
            b, q = i // 4, i % 4
            ch = slice(q * 256, (q + 1) * 256)
            maps.append(rwkv_inputs(colsT[b][O_RW:O_SWA], ch, p, l == 1,
                                    xvT=colsT[b][O_XV:O_XV + 32] if l == 1 else None,
                                    vfirst=vfirst[b][:, ch] if l == 1 else None))
        res = run(nc, maps)
        if l == 0:
            vfirst = [np.zeros((T, 1024), np.float32) for _ in range(B)]
        for i in range(NCORES):
            b, q = i // 4, i % 4
            mixT[b][2048 + q * 256:2048 + (q + 1) * 256] = np.asarray(res[i]["otok"]).T
            if l == 0:
                vfirst[b][:, q * 256:(q + 1) * 256] = res[i]["vfirst"]
        nc = prog(("swa", T), lambda: build_swa(T))
        maps = []
        for i in range(NCORES):
            b, q = i // 4, i % 4
            cs_ = colsT[b][O_SWA:O_SWA + 1280]
            kvh = q // 2
            bt, gm = swa_consts(rel_bias[:, 8 + 4 * q:8 + 4 * q + 4])
            par = np.zeros((64, 8), np.float32)
            par[:, 0] = f(swa_q_norm[l])
            par[:, 1] = f(swa_k_norm[l])
            par[:, 2:6] = f(swa_sinks[l])[None, 4 * q:4 * q + 4]
            maps.append({"q4T": np.ascontiguousarray(cs_[q * 256:(q + 1) * 256].reshape(4, 64, T).transpose(1, 0, 2)),
                         "kT": np.ascontiguousarray(cs_[1024 + kvh * 64:1024 + (kvh + 1) * 64]),
                         "vtok": np.ascontiguousarray(cs_[1152 + kvh * 64:1152 + (kvh + 1) * 64].T),
                         "par": par, "bias": bt, "gm": gm})
        res = run(nc, maps)
        for i in range(NCORES):
            b, q = i // 4, i % 4
            mixT[b][3072 + q * 256:3072 + (q + 1) * 256] = np.asarray(res[i]["oT"]).transpose(1, 0, 2).reshape(256, T)
        del colsT
        nc = prog(("ffn", T), lambda: build_ffn(TOKC, NFT=ffn_w1.shape[2] // 128))
        maps = []
        for i in range(NCORES):
            b, tq = i // 4, i % 4
            ts = slice(tq * TOKC, (tq + 1) * TOKC)
            maps.append({"xT": np.ascontiguousarray(xT[b][:, ts]), "mixT": np.ascontiguousarray(mixT[b][:, ts]), "mod": mods[b],
                         "wo": w_out_b, "w1": w1_b, "w3": w3_b, "w2": w2_b})
        res = run(nc, maps)
        xT = [np.concatenate([res[b * 4 + tq]["outT"] for tq in range(4)], axis=1) for b in range(B)]
        del res, maps
    return np.stack([np.ascontiguousarray(xT[b].T) for b in range(B)]).astype(np.float32)


G8 = [[0, 1, 2, 3, 4, 5, 6, 7]]
NFM, NTM = 1920, 1216
TM_BLOCKS = ((0, 512), (512, 512), (1024, 192))


def emit_cast(k, src, dst, M, CH, bufs):
    ins, outs = bufs
    assert M % CH == 0 and CH <= 4096
    for i in range(M // CH):
        a, b = ins.next(), outs.next()
        k.dma("sp", a[:, 0:CH], src[:, i * CH:(i + 1) * CH], w=[a])
        if i % 2 == 0:
            k.op("dve", lambda e: e.tensor_copy(out=b[:, 0:CH], in_=a[:, 0:CH]), r=[a], w=[b])
        else:
            k.op("act", lambda e: e.activation(out=b[:, 0:CH], in_=a[:, 0:CH], func=AF.Copy), r=[a], w=[b])
        k.dma("pool", dst[:, i * CH:(i + 1) * CH], b[:, 0:CH], r=[b], w=[dst])


def emit_ada(k, cT, w, bias, out, NCOL=6144):
    ct = k.sb("ct", [128, 64], F32)
    st = k.sb("st", [128, 64], F32)
    bt = k.sb("bt", [2, NCOL], F32)
    ot = k.sb("ot", [2, NCOL], F32)
    k.dma("sp", ct[:, :], cT[:, :], w=[ct])
    k.dma("sp", bt[:, :], bias[:, :], w=[bt])
    k.op("act", lambda e: e.activation(out=st[:, :], in_=ct[:, :], func=AF.Silu), r=[ct], w=[st])
    wv = w.h.rearrange("(kc p) n -> p kc n", p=128)
    wts = Rot([k.sb("wt", [128, 8, 512], F32) for _ in range(3)])
    pss = Rot([k.ps("pa", [2, 512]) for _ in range(2)])
    for nb in range(NCOL // 512):
        ps = pss.next()
        for g in range(4):
            wt = wts.next()
            k.dma("sp" if g % 2 == 0 else "pool", wt[:, :, :], wv[:, g * 8:(g + 1) * 8, nb * 512:(nb + 1) * 512], w=[wt])
            for j in range(8):
                kc = g * 8 + j
                k.mm(ps, ps[:, :], st, st[:, kc * 2:kc * 2 + 2], wt, wt[:, j, :], kc == 0, kc == 31)
        sl = slice(nb * 512, (nb + 1) * 512)
        k.op("dve", lambda e: e.tensor_tensor(out=ot[:, sl], in0=ps[:, :], in1=bt[:, sl], op=ALU.add), r=[ps, bt], w=[ot])
    k.dma("sp", out[:, :], ot[:, :], r=[ot], w=[out])


def build_fused(Tn=8192, NFT=DFF // 128):
    TOKC = Tn // 4
    NB = TOKC // 512
    k = K()
    IN = lambda n, shp, dt=F32: k.dram(n, shp, dt, "in")
    x_in = IN("xT", [D, TOKC])
    sel = IN("sel", [128, 16])
    cT, ada_w, ada_b = IN("cT", [128, 64]), IN("ada_w", [D, 6144]), IN("ada_b", [2, 6144])
    identd = IN("ident", [128, 128])
    WM = {"wo": 16384, "w1": NFT * 512, "w3": NFT * 512, "w2": NFT * 512}
    WCH = {"wo": 4096, "w1": NFT * 32, "w3": NFT * 32, "w2": NFT * 32}
    L = []
    for l in range(2):
        d = {"wsh": {n: IN("%s_%d" % (n, l), [128, WM[n]]) for n in WM},
             "wfm": IN("wfm_%d" % l, [128, 15 * 4096]), "wtm": IN("wtm_%d" % l, [128, 32 * NTM]),
             "d_par": IN("d_par_%d" % l, [128, 6]), "d_lam": IN("d_lam_%d" % l, [128, 256]),
             "d_bias": IN("d_bias_%d" % l, [2, 128, 5, 512]), "d_gm": IN("d_gm_%d" % l, [128, 128]),
             "g_gup": IN("g_gup_%d" % l, [16, 128]), "g_cst": IN("g_cst_%d" % l, [128, 768]),
             "r_pfm": IN("r_pfm_%d" % l, [64, 1728]), "r_ptm": IN("r_ptm_%d" % l, [64, 1280]),
             "r_mats": IN("r_mats_%d" % l, [64, 768]), "r_cst": IN("r_cst_%d" % l, [64, 1280]),
             "s_par": IN("s_par_%d" % l, [64, 8]), "s_bias": IN("s_bias_%d" % l, [128, 2, 512]), "s_gm": IN("s_gm_%d" % l, [64, 64])}
        if l == 1:
            d["r_vmat"] = IN("r_vmat_1", [32, 320])
        d["wb"] = {n: k.dint("wb_%s_%d" % (n, l), [128, WM[n]], BF16) for n in WM}
        d["wf"] = {n: k.dint("wf_%s_%d" % (n, l), [1024, WM[n]], BF16) for n in WM}
        d["wfm_b"] = k.dint("wfmb_%d" % l, [128, 15 * 4096], BF16)
        d["wtm_b"] = k.dint("wtmb_%d" % l, [128, 32 * NTM], BF16)
        L.append(d)
    out = k.dram("outT", [D, TOKC], F32, "out")
    adab, adag = k.dint("adab", [2, 6144], F32), k.dint("adag", [16, 6144], F32)
    hb, hg = k.dint("hb", [D, TOKC], BF16), k.dint("hg", [8 * D, TOKC], BF16)
    colsF, colsTM = k.dint("colsF", [NFM, Tn + 1], F32), k.dint("colsTM", [Tn + 1, NTM], F32)
    mixq, mixg = k.dint("mixq", [1024, Tn], BF16), k.dint("mixg", [8 * 1024, Tn], BF16)
    vfirst = k.dint("vfirst", [Tn, 256], F32)
    x1T, xmid = k.dint("x1T", [D, TOKC], F32), k.dint("xmid", [D, TOKC], F32)

    selt = k.sb("selt", [128, 16], F32)
    k.dma("sp", selt[:, :], sel[:, :], w=[selt])
    identf = k.sb("identf", [128, 128], F32)
    k.dma("sp", identf[:, :], identd[:, :], w=[identf])
    mts = [k.sb("mt%d" % l, [128, 192], F32) for l in range(2)]
    for e_ in (EPS, 1.0, LN_EPS):
        eps_ap(k, e_, selt[:, 0:1])

    k.push()
    emit_ada(k, cT, ada_w, ada_b, adab)
    k.pop()
    k.cc("AllGather", G8, adab, adab[:, :], adag, adag[:, :])
    k.barrier()
    k.push()
    pst = k.ps("pst", [128, 96])
    for l in range(2):
        for half in range(2):
            cands = []
            for bb in range(2):
                ta = k.sb("ta", [96, 128], F32)
                for j2 in range(2):
                    qq = 2 * half + j2
                    row = 2 * (4 * l + qq) + bb
                    k.dma("sp", ta[j2 * 48:(j2 + 1) * 48, :], adag.h[row, :].rearrange("(r p) -> r p", p=128), r=[adag], w=[ta])
                cands.append(ta)
            k.op("dve", lambda e: e.tensor_scalar_mul(out=cands[1][:, :], in0=cands[1][:, :], scalar1=selt[0:96, 1:2]), r=[cands[1], selt], w=[cands[1]])
            k.op("dve", lambda e: e.scalar_tensor_tensor(out=cands[0][:, :], in0=cands[0][:, :], scalar=selt[0:96, 0:1], in1=cands[1][:, :],
                                                         op0=ALU.mult, op1=ALU.add), r=[cands[0], cands[1], selt], w=[cands[0]])
            k.mm(pst, pst[:, :], cands[0], cands[0][:, :], identf, identf[0:96, 0:96], True, True)
            k.op("dve", lambda e: e.tensor_copy(out=mts[l][:, half * 96:(half + 1) * 96], in_=pst[:, :]), r=[pst], w=[mts[l]])
        for comp in (1, 4):
            sl = slice(comp * 32, (comp + 1) * 32)
            k.op("dve", lambda e: e.tensor_scalar_add(out=mts[l][:, sl], in0=mts[l][:, sl], scalar1=1.0), r=[mts[l]], w=[mts[l]])
    k.pop()

    def cast_win(l):
        k.push()
        cbufs = (Rot([k.sb("cin", [128, 4096], F32) for _ in range(3)]), Rot([k.sb("cout", [128, 4096], BF16) for _ in range(3)]))
        emit_cast(k, L[l]["wfm"], L[l]["wfm_b"], 15 * 4096, 4096, cbufs)
        emit_cast(k, L[l]["wtm"], L[l]["wtm_b"], 32 * NTM, NTM * 2, cbufs)
        k.pop()

    WN = ("wo", "w1", "w3", "w2")

    def cast_shared(l):
        k.push()
        cbufs = (Rot([k.sb("cin", [128, 4096], F32) for _ in range(3)]), Rot([k.sb("cout", [128, 4096], BF16) for _ in range(3)]))
        for n in WN:
            emit_cast(k, L[l]["wsh"][n], L[l]["wb"][n], WM[n], WCH[n], cbufs)
        k.pop()

    def gather_shared(l, n):
        k.cc("AllGather", G8, L[l]["wb"][n], L[l]["wb"][n][:, :], L[l]["wf"][n], L[l]["wf"][n][:, :])

    cast_win(0)
    k.aux = "dve"

    class _FlatTiles:
        def __init__(self, ap, c):
            self.tensor, self.c = ap.tensor, c

        def __getitem__(self, nt):
            return bass.AP(self.tensor, nt * 128 * self.c, [[self.c, 128], [1, self.c]])

    def tiled(t_, c):
        return t_.view(_FlatTiles(t_.h, c))

    for l in range(2):
        P = L[l]
        xsrc = x_in if l == 0 else xmid
        xdst = xmid if l == 0 else out
        mt = mts[l]
        k.push()
        ones_mean = make_ones(k, 1.0 / D)
        xv_ = xsrc.h.rearrange("(kc p) t -> p kc t", p=128)
        hbv = hb.h.rearrange("(kc p) t -> p kc t", p=128)
        xt = k.sb("xt", [128, 32, 512], F32)
        hT = k.sb("hT", [128, 32, 512], BF16)
        rstd = k.sb("rstd", [128, 512], F32)
        ps_stat = k.ps("ps_stat", [128, 512])
        sqs = Rot([k.sb("sq", [128, 512], F32) for _ in range(2)])
        tmps = Rot([k.sb("tmp", [128, 512], F32) for _ in range(2)])
        for tb in range(NB):
            ts = slice(tb * 512, (tb + 1) * 512)
            for g in range(4):
                k.dma("sp" if g % 2 == 0 else "pool", xt[:, g * 8:(g + 1) * 8, :], xv_[:, g * 8:(g + 1) * 8, ts], r=[xsrc], w=[xt])
            emit_norm_mod(k, 512, lambda kc: (xt, xt[:, kc, :]), hT, lambda kc: mt[:, 32 + kc:33 + kc], lambda kc: mt[:, kc:kc + 1],
                          ones_mean, ps_stat, sqs, tmps, rstd)
            for g in range(4):
                k.dma("pool", hbv[:, g * 8:(g + 1) * 8, ts], hT[:, g * 8:(g + 1) * 8, :], r=[hT], w=[hb])
        k.pop()
        k.cc("AllGather", G8, hb, hb[:, :], hg, hg[:, :])
        k.barrier()
        k.push()
        zt = k.sb("zt", [128, NTM], F32)
        k.op(k.aux, lambda e: e.memset(zt[:, :], 0.0), w=[zt])
        for nt in range(NFM // 128):
            k.dma("sp", colsF.h[nt * 128:(nt + 1) * 128, 0:1], zt[:, 0:1], r=[zt], w=[colsF], slow=True)
        k.dma("sp", colsTM.h[0:1, :], zt[0:1, :], r=[zt], w=[colsTM])
        hTs = Rot([k.sb("hT", [128, 32, 512], BF16) for _ in range(2)])
        cst_ = [Rot([k.sb("cand%d" % c, [128, 8, 512], BF16) for _ in range(2)]) for c in range(2)]
        wfs = Rot([k.sb("wf", [128, 32, 128], BF16) for _ in range(3)])
        wms = Rot([k.sb("wm", [128, 32, 512], BF16) for _ in range(2)])
        pss = Rot([k.ps("pg", [128, 512]) for _ in range(6)])
        evs = Rot([k.sb("ev", [128, 512], F32) for _ in range(4)])
        wfm_v = tiled(P["wfm_b"], 4096)
        wtm_v = P["wtm_b"].h.rearrange("p (kc n) -> p kc n", n=NTM)
        ne = 0

        def select_h(tq, j):
            hT = hTs.next()
            for g in range(4):
                cs_ = []
                for c in range(2):
                    ct_ = cst_[c].next()
                    r0 = (4 * c + tq) * D + g * 1024
                    k.dma("sp", ct_[:, :, :], hg.h[r0:r0 + 1024, j * 512:(j + 1) * 512].rearrange("(kc p) t -> p kc t", p=128),
                          r=[hg], w=[ct_])
                    cs_.append(ct_)
                k.op(k.aux, lambda e: e.tensor_scalar_mul(out=cs_[1][:, :, :], in0=cs_[1][:, :, :], scalar1=selt[:, 1:2]),
                     r=[cs_[1], selt], w=[cs_[1]])
                k.op("dve", lambda e: e.scalar_tensor_tensor(out=hT[:, g * 8:(g + 1) * 8, :], in0=cs_[0][:, :, :], scalar=selt[:, 0:1],
                                                             in1=cs_[1][:, :, :], op0=ALU.mult, op1=ALU.add),
                     r=[cs_[0], cs_[1], selt], w=[hT])
            return hT

        blocks = [(tq, j) for tq in range(4) for j in range(NB)]
        nxt = select_h(*blocks[0])
        for bi, (tq, j) in enumerate(blocks):
            if True:
                tok0 = tq * TOKC + j * 512
                hT = nxt
                if bi + 1 < len(blocks):
                    nxt = select_h(*blocks[bi + 1])
                for nt in range(NFM // 128):
                    wt = wfs.next()
                    k.dma("sp", wt[:, :, :], wfm_v.h[nt].rearrange("p (kc n) -> p kc n", n=128), r=[P["wfm_b"]], w=[wt])
                    ps = pss.next()
                    for kc in range(32):
                        k.mm(ps, ps[:, :], wt, wt[:, kc, :], hT, hT[:, kc, :], kc == 0, kc == 31)
                    ev = evs.next()
                    ne += 1
                    if ne % 2 == 0:
                        k.op("dve", lambda e: e.tensor_copy(out=ev[:, :], in_=ps[:, :]), r=[ps], w=[ev])
                    else:
                        k.op("act", lambda e: e.activation(out=ev[:, :], in_=ps[:, :], func=AF.Copy), r=[ps], w=[ev])
                    k.dma(k.stq, colsF.h[nt * 128:(nt + 1) * 128, 1 + tok0:1 + tok0 + 512], ev[:, :], r=[ev], w=[colsF])
                for (c0_, n_) in TM_BLOCKS:
                    wm = wms.next()
                    k.dma("sp", wm[:, :, 0:n_], wtm_v[:, :, c0_:c0_ + n_], r=[P["wtm_b"]], w=[wm])
                    for tt in range(4):
                        ps = pss.next()
                        for kc in range(32):
                            k.mm(ps, ps[:, 0:n_], hT, hT[:, kc, tt * 128:(tt + 1) * 128], wm, wm[:, kc, 0:n_], kc == 0, kc == 31)
                        ev = evs.next()
                        ne += 1
                        if ne % 2 == 0:
                            k.op("dve", lambda e: e.tensor_copy(out=ev[:, 0:n_], in_=ps[:, 0:n_]), r=[ps], w=[ev])
                        else:
                            k.op("act", lambda e: e.activation(out=ev[:, 0:n_], in_=ps[:, 0:n_], func=AF.Copy), r=[ps], w=[ev])
                        r1 = 1 + tok0 + tt * 128
                        k.dma(k.stq, colsTM.h[r1:r1 + 128, c0_:c0_ + n_], ev[:, 0:n_], r=[ev], w=[colsTM])
        k.pop()
        if l == 0:
            cast_win(1)
            cast_shared(0)
            cast_shared(1)
            for n in WN:
                gather_shared(0, n)
        Fv = lambda a: colsF.view(a)
        Tv = lambda a: colsTM.view(a)
        k.push()
        emit_diff(k, Tn, 2, l, {"qT": Fv(colsF.h[0:256, 1:]), "kT": Fv(colsF.h[256:512, 1:]),
                                "vtok": [Tv(colsTM.h[1:, h * 128:(h + 1) * 128]) for h in range(2)],
                                "par": P["d_par"], "lam": P["d_lam"], "bias": P["d_bias"], "gm": P["d_gm"],
                                "oT": mixq.view(mixq.h[0:256, :])})
        k.pop()
        k.push()
        emit_swa(k, Tn, {"q4T": Fv(colsF.h[1472:1728, 1:].rearrange("(h p) t -> p h t", p=64)), "kT": Fv(colsF.h[1728:1792, 1:]),
                         "vtok": Tv(colsTM.h[1:, 1152:1216]), "par": P["s_par"], "bias": P["s_bias"], "gm": P["s_gm"],
                         "oT": mixq.view(mixq.h[768:1024, :].rearrange("(h p) t -> p h t", p=64))})
        k.pop()
        k.push()
        emit_gla(k, Tn, {"qT": Fv(colsF.h[512:640, 1:]), "kT": Fv(colsF.h[640:768, 1:]), "ktok": Tv(colsTM.h[1:, 256:384]),
                         "vtok": Tv(colsTM.h[1:, 384:640]), "gtok": Tv(colsTM.h[1:, 640:896]), "gdT": Fv(colsF.h[1792:1808, 1:]),
                         "gup": P["g_gup"], "cst": P["g_cst"], "oT_fm": mixq.view(mixq.h[256:512, :]), "ident": identd})
        k.pop()
        k.push()
        rio = {"rk": Fv(colsF.h[768:1280, :].rearrange("(a p) t -> p a t", p=64)),
               "lo": Fv(colsF.h[1280:1472, :].rearrange("(a p) t -> p a t", p=64)),
               "vt": Tv(colsTM.h[:, 896:1152]), "pfm": P["r_pfm"], "ptm": P["r_ptm"], "mats": P["r_mats"], "cst": P["r_cst"],
               "oT_fm": mixq.view(mixq.h[512:768, :])}
        if l == 1:
            rio.update({"xv": Fv(colsF.h[1808:1840, :]), "vf": vfirst, "vmat": P["r_vmat"]})
        else:
            rio["vfirst"] = vfirst
        emit_rwkv(k, Tn, l == 1, rio)
        k.pop()
        k.cc("AllGather", G8, mixq, mixq[:, :], mixg, mixg[:, :])
        k.barrier()
        k.push()
        cnd = Rot([k.sb("cnd", [128, 8, 512], BF16) for _ in range(3)])

        def load_mix(tb, hT_):
            n = 0
            for g in range(4):
                for c in range(8):
                    bb, tq_ = c // 4, c % 4
                    ct_ = cnd.next()
                    r0 = (4 * bb + g) * 1024
                    c0_ = tq_ * TOKC + tb * 512
                    k.dma("sp", ct_[:, :, :], mixg.h[r0:r0 + 1024, c0_:c0_ + 512].rearrange("(kc p) t -> p kc t", p=128), r=[mixg], w=[ct_])
                    dst = hT_[:, g * 8:(g + 1) * 8, :]
                    eng = "dve"
                    n += 1
                    if c == 0:
                        k.op(eng, lambda e: e.tensor_scalar_mul(out=dst, in0=ct_[:, :, :], scalar1=selt[:, 2:3]), r=[ct_, selt], w=[hT_])
                    else:
                        k.op(eng, lambda e: e.scalar_tensor_tensor(out=dst, in0=ct_[:, :, :], scalar=selt[:, 2 + c:3 + c], in1=dst,
                                                                   op0=ALU.mult, op1=ALU.add), r=[ct_, selt, hT_], w=[hT_])

        emit_ffn(k, TOKC, 512, NFT, {"xT": xsrc, "wo": tiled(P["wf"]["wo"], 4096), "w1": tiled(P["wf"]["w1"], 4096),
                                    "w3": tiled(P["wf"]["w3"], 4096), "w2": tiled(P["wf"]["w2"], NFT * 128),
                                    "x1T": x1T, "outT": xdst, "mt": mt, "load_mix": load_mix,
                                    "hook": (lambda tb: gather_shared(1, WN[tb])) if (l == 0 and NB == 4) else
                                            ((lambda tb: [gather_shared(1, n) for n in WN] if tb == 0 else None) if l == 0 else None)})
        k.pop()
    return k.finish()


def kernel(x, c, rel_bias, ada_w, ada_b, w_in, w_out, diff_q_norm, diff_k_norm, diff_lambda, diff_subln,
           gla_gate_up, gla_gate_bias, gla_out_norm, rwkv_mu, rwkv_w_up, rwkv_w0, rwkv_a_up, rwkv_a0, rwkv_g_up,
           rwkv_k_k, rwkv_k_a, rwkv_r_k, rwkv_ln_w, rwkv_ln_b, rwkv_vres_down, rwkv_vres_mu, rwkv_vres_up, rwkv_v0,
           swa_q_norm, swa_k_norm, swa_sinks, ffn_w1, ffn_w3, ffn_w2):
    f = lambda a: np.asarray(a, dtype=np.float32)
    x, c, rel_bias = f(x), f(c), f(rel_bias)
    B, T = x.shape[0], x.shape[1]
    TOKC = T // 4
    NFT = ffn_w1.shape[2] // 128
    nc = prog(("fused", T, NFT), lambda: build_fused(T, NFT))
    cT = np.ascontiguousarray(c.reshape(2, 32, 128).transpose(2, 1, 0)).reshape(128, 64)
    ident = np.eye(128, dtype=np.float32)
    perm = np.array([piece * 1024 + qq * 256 + cc for qq in range(4) for piece in range(4) for cc in range(256)])
    ar = np.arange
    shares, percore = [], [dict() for _ in range(NCORES)]
    for l in range(2):
        flat = {"wo": tile_weight(f(w_out[l])[perm, :]).reshape(-1), "w1": tile_weight(f(ffn_w1[l])).reshape(-1),
                "w3": tile_weight(f(ffn_w3[l])).reshape(-1), "w2": tile_weight(f(ffn_w2[l])).reshape(-1)}
        w_ext = np.concatenate([f(w_in[l]), f(rwkv_vres_down[0]) if l == 1 else np.zeros((D, 32), np.float32)], axis=1)
        p = {"rwkv_mu": f(rwkv_mu[l]), "rwkv_w_up": f(rwkv_w_up[l]), "rwkv_w0": f(rwkv_w0[l]), "rwkv_a_up": f(rwkv_a_up[l]),
             "rwkv_a0": f(rwkv_a0[l]), "rwkv_g_up": f(rwkv_g_up[l]), "rwkv_k_k": f(rwkv_k_k[l]), "rwkv_k_a": f(rwkv_k_a[l]),
             "rwkv_r_k": f(rwkv_r_k[l]), "rwkv_ln_w": f(rwkv_ln_w[l]), "rwkv_ln_b": f(rwkv_ln_b[l])}
        if l == 1:
            p.update({"rwkv_vres_mu": f(rwkv_vres_mu[0]), "rwkv_vres_up": f(rwkv_vres_up[0]), "rwkv_v0": f(rwkv_v0[0])})
        dummyT = np.zeros((3264, 1), np.float32)
        for i in range(NCORES):
            b, q = i // 4, i % 4
            d = percore[i]
            for n, a in flat.items():
                sz = a.size // NCORES
                d["%s_%d" % (n, l)] = a[i * sz:(i + 1) * sz].reshape(128, sz // 128)
            fm_cols = np.concatenate([q * 256 + ar(256), 1024 + q * 256 + ar(256), O_GLA + q * 128 + ar(128),
                                      O_GLA + 512 + q * 128 + ar(128), O_RW + q * 256 + ar(256), O_RW + 1024 + q * 256 + ar(256),
                                      O_RW + 3072 + ar(192), O_SWA + q * 256 + ar(256), O_SWA + 1024 + (q // 2) * 64 + ar(64),
                                      O_GLA + 3072 + ar(16), O_XV + ar(32)])
            wfm = np.zeros((D, NFM), np.float32)
            wfm[:, :fm_cols.size] = w_ext[:, fm_cols]
            d["wfm_%d" % l] = tile_weight(wfm).reshape(128, 15 * 4096)
            tm_cols = np.concatenate([2048 + q * 256 + ar(256), O_GLA + 512 + q * 128 + ar(128), O_GLA + 1024 + q * 256 + ar(256),
                                      O_GLA + 2048 + q * 256 + ar(256), O_RW + 2048 + q * 256 + ar(256),
                                      O_SWA + 1152 + (q // 2) * 64 + ar(64)])
            d["wtm_%d" % l] = np.ascontiguousarray(w_ext[:, tm_cols].reshape(32, 128, NTM).transpose(1, 0, 2)).reshape(128, 32 * NTM)
            bt, far, gm = diff_consts(rel_bias[:, 2 * q:2 * q + 2])
            par = np.zeros((128, 6), np.float32)
            par[:, 0] = np.tile(f(diff_q_norm[l]), 2)
            par[:, 1] = np.tile(f(diff_k_norm[l]), 2)
            par[:, 2] = f(diff_subln[l])
            par[:, 4:6] = far[None, :]
            d["d_par_%d" % l], d["d_bias_%d" % l], d["d_gm_%d" % l] = par, bt, gm
            d["d_lam_%d" % l] = np.ascontiguousarray(np.broadcast_to(f(diff_lambda[l]).reshape(1, 256), (128, 256)))
            d["g_gup_%d" % l] = np.ascontiguousarray(f(gla_gate_up[l])[:, q * 128:(q + 1) * 128])
            d["g_cst_%d" % l] = gla_consts(f(gla_gate_bias[l])[q * 128:(q + 1) * 128], f(gla_out_norm[l]))
            ri = rwkv_inputs(dummyT, slice(q * 256, (q + 1) * 256), p, l == 1, xvT=np.zeros((32, 1), np.float32),
                             vfirst=np.zeros((1, 256), np.float32))
            d["r_pfm_%d" % l], d["r_ptm_%d" % l], d["r_mats_%d" % l], d["r_cst_%d" % l] = ri["pfm"], ri["ptm"], ri["mats"], ri["cst"]
            if l == 1:
                d["r_vmat_1"] = ri["vmat"]
            sb_, sgm = swa_consts(rel_bias[:, 8 + 4 * q:8 + 4 * q + 4])
            spar = np.zeros((64, 8), np.float32)
            spar[:, 0] = f(swa_q_norm[l])
            spar[:, 1] = f(swa_k_norm[l])
            spar[:, 2:6] = f(swa_sinks[l])[None, 4 * q:4 * q + 4]
            d["s_par_%d" % l], d["s_bias_%d" % l], d["s_gm_%d" % l] = spar, sb_, sgm
        del flat, w_ext
    for i in range(NCORES):
        b, q = i // 4, i % 4
        d = percore[i]
        d["xT"] = np.ascontiguousarray(x[b, q * TOKC:(q + 1) * TOKC].T)
        sel = np.zeros((128, 16), np.float32)
        sel[:, b] = 1.0
        sel[:, 2 + i] = 1.0
        d["sel"] = sel
        d["cT"] = cT
        la, qa = i // 4, i % 4
        d["ada_w"] = np.ascontiguousarray(f(ada_w[la])[:, qa * 6144:(qa + 1) * 6144])
        d["ada_b"] = np.ascontiguousarray(np.broadcast_to(f(ada_b[la])[qa * 6144:(qa + 1) * 6144], (2, 6144)))
        d["ident"] = ident
        for kk_ in d:
            d[kk_] = np.ascontiguousarray(d[kk_], dtype=np.float32)
    res = run(nc, percore)
    outp = np.zeros((B, T, D), np.float32)
    for i in range(NCORES):
        b, q = i // 4, i % 4
        outp[b, q * TOKC:(q + 1) * TOKC] = np.asarray(res[i]["outT"]).T
    return outp
```
